# Optimizing a Trainium2 kernel written in Bass

```python
import math
import jax, jax.numpy as jnp
from jax import lax
import numpy as np

D_MODEL = 4096
BATCH = 2
SEQ = 8192
DEPTH = 2

GRID_W = 64
CTX_LEN = 256
N_MIXERS = 4
MIX_W = D_MODEL
GROUP_W = MIX_W // N_MIXERS
FNET_HEADS = 4
FNET_HEAD_DIM = GROUP_W // FNET_HEADS
S5_GROUP_CH = 16
S5_GROUPS = GROUP_W // S5_GROUP_CH
S5_STATE = 64
CONV_WIDTH = 31
SSD_HEAD_DIM = 64
SSD_HEADS = GROUP_W // SSD_HEAD_DIM
SSD_NGROUPS = 4
SSD_STATE = 128
SSD_CONV = 5
SSD_CHUNK = 128
SSD_XBC = GROUP_W + 2 * SSD_NGROUPS * SSD_STATE
SSD_IN = GROUP_W + SSD_XBC + 2 * SSD_HEADS
OFF_A = 0
OFF_S5 = GROUP_W
OFF_CV = 2 * GROUP_W
OFF_SSD = 4 * GROUP_W
IN_COLS = OFF_SSD + SSD_IN
FFN_HIDDEN = ((8 * D_MODEL + 2) // 3 + 255) // 256 * 256
ALPHA = (2 * DEPTH) ** 0.25
BETA = (8 * DEPTH) ** -0.25
LN_EPS = 1e-5
RMS_EPS = 1e-5

kernel_name = "hybrid_fnet_s5_conformer_ssd_dit"

F32 = jnp.float32


def layer_norm(x, g, b):
    xf = x.astype(F32)
    mu = jnp.mean(xf, -1, keepdims=True)
    var = jnp.mean(jnp.square(xf - mu), -1, keepdims=True)
    return ((xf - mu) * lax.rsqrt(var + LN_EPS) * g + b).astype(x.dtype)


def pos_2d(rows, dim):
    t = jnp.arange(rows * GRID_W)
    r = (t // GRID_W).astype(F32)[:, None]
    col = (t % GRID_W).astype(F32)[:, None]
    q = dim // 4
    omega = jnp.exp(-math.log(10000.0) * jnp.arange(q, dtype=F32) / q)[None]
    return jnp.concatenate([jnp.sin(r * omega), jnp.cos(r * omega),
                            jnp.sin(col * omega), jnp.cos(col * omega)], -1)


def depthwise_conv(x, w, bias):
    k = w.shape[0]
    pad = (k - 1) // 2
    y = lax.conv_general_dilated(x, w[:, None, :].astype(x.dtype), (1,), [(pad, k - 1 - pad)],
                                 dimension_numbers=("NWC", "WIO", "NWC"),
                                 feature_group_count=x.shape[-1])
    return y + bias.astype(x.dtype)


def fourier_mix(ua):
    b, l, _ = ua.shape
    u = ua.astype(F32).reshape(b, l, FNET_HEADS, FNET_HEAD_DIM)
    y = jnp.fft.fft2(u, axes=(1, 3), norm="ortho").real
    return y.reshape(b, l, GROUP_W)


def _lin_combine(e1, e2):
    a1, b1 = e1
    a2, b2 = e2
    return a1 * a2, a2 * b1 + b2


def s5_mix(us, p, h0_f, h0_b):
    b, l, _ = us.shape
    u = us.astype(F32).reshape(b, l, S5_GROUPS, S5_GROUP_CH)
    bmat = lax.complex(p["s5_b_re"].astype(F32), p["s5_b_im"].astype(F32))
    cmat = lax.complex(p["s5_c_re"].astype(F32), p["s5_c_im"].astype(F32))
    bu = jnp.einsum("blgc,gpc->blgp", u, bmat)

    def direction(k, h0, reverse):
        lam = lax.complex(p["s5_lam_re"][k].astype(F32), p["s5_lam_im"][k].astype(F32))
        dt = jnp.exp(p["s5_log_dt"][k].astype(F32))[:, None]
        a_bar = jnp.exp(lam * dt)
        inp = bu * ((a_bar - 1.0) / lam)
        edge = -1 if reverse else 0
        inp = inp.at[:, edge].add(a_bar * h0)
        a = jnp.broadcast_to(a_bar, inp.shape)
        _, h = lax.associative_scan(_lin_combine, (a, inp), axis=1, reverse=reverse)
        return h

    h_f = direction(0, h0_f, False)
    h_b = direction(1, h0_b, True)
    y = jnp.einsum("blgp,gcp->blgc", h_f + h_b, cmat).real
    y = y + p["s5_d"].astype(F32).reshape(S5_GROUPS, S5_GROUP_CH) * u
    g = jax.nn.gelu(y.reshape(b, l, GROUP_W))
    out = g * jax.nn.sigmoid(g @ p["s5_w_glu"].astype(F32))
    return out, h_f[:, -1], h_b[:, 0]


def conv_module(uc, p):
    v, gate = jnp.split(uc.astype(F32), 2, axis=-1)
    h = v * jax.nn.sigmoid(gate)
    h = depthwise_conv(h, p["cv_w"].astype(F32), p["cv_b"].astype(F32))
    h = layer_norm(h, p["cv_ln_g"], p["cv_ln_b"])
    return jax.nn.silu(h)


def ssd_chunked(xs, dt, a_head, bm, cm, h0):
    b, l, nh, hp = xs.shape
    g, n = bm.shape[2], bm.shape[3]
    e = nh // g
    q = SSD_CHUNK
    c = l // q
    x = xs.reshape(b, c, q, g, e, hp)
    dt = dt.reshape(b, c, q, g, e)
    bc = bm.reshape(b, c, q, g, n)
    cc = cm.reshape(b, c, q, g, n)
    a_cum = jnp.cumsum(dt * a_head.reshape(g, e), axis=2)
    seg = a_cum[:, :, :, None] - a_cum[:, :, None]
    mask = jnp.tril(jnp.ones((q, q), bool))[:, :, None, None]
    decay = jnp.where(mask, jnp.exp(jnp.where(mask, seg, 0.0)), 0.0)
    scores = jnp.einsum("bcign,bcjgn->bcijg", cc, bc)
    w = scores[..., None] * decay * dt[:, :, None]
    y_diag = jnp.einsum("bcijge,bcjgep->bcigep", w, x)
    a_last = a_cum[:, :, -1]
    decay_end = jnp.exp(a_last[:, :, None] - a_cum)
    states = jnp.einsum("bcjgn,bcjge,bcjgep->bcgepn", bc, decay_end * dt, x)

    def step(s, inp):
        dec, st = inp
        return dec[..., None, None] * s + st, s

    final, prev = lax.scan(step, h0, (jnp.moveaxis(jnp.exp(a_last), 1, 0),
                                      jnp.moveaxis(states, 1, 0)))
    prev = jnp.moveaxis(prev, 0, 1)
    y_off = jnp.einsum("bcign,bcgepn,bcige->bcigep", cc, prev, jnp.exp(a_cum))
    return (y_diag + y_off).reshape(b, l, nh, hp), final


def ssd_mix(ud, p, h0_f, h0_b):
    b, l, _ = ud.shape
    ud = ud.astype(F32)
    z = ud[..., :GROUP_W]
    xbc = jax.nn.silu(depthwise_conv(ud[..., GROUP_W:GROUP_W + SSD_XBC],
                                     p["ssd_conv_w"].astype(F32), p["ssd_conv_b"].astype(F32)))
    dt_raw = ud[..., GROUP_W + SSD_XBC:].reshape(b, l, 2, SSD_HEADS)
    nb = SSD_NGROUPS * SSD_STATE
    xs = xbc[..., :GROUP_W].reshape(b, l, SSD_HEADS, SSD_HEAD_DIM)
    bm = xbc[..., GROUP_W:GROUP_W + nb].reshape(b, l, SSD_NGROUPS, SSD_STATE)
    cm = xbc[..., GROUP_W + nb:].reshape(b, l, SSD_NGROUPS, SSD_STATE)
    dt = jax.nn.softplus(dt_raw + p["ssd_dt_bias"].astype(F32))
    a_head = -jnp.exp(p["ssd_a_log"].astype(F32))
    y_f, s_f = ssd_chunked(xs, dt[:, :, 0], a_head[0], bm, cm, h0_f)
    fl = lambda t: jnp.flip(t, axis=1)
    y_b, s_b = ssd_chunked(fl(xs), fl(dt[:, :, 1]), a_head[1], fl(bm), fl(cm), h0_b)
    y = y_f + fl(y_b) + p["ssd_d"].astype(F32)[:, None] * xs
    t = (y.reshape(b, l, GROUP_W) * jax.nn.silu(z)).reshape(b, l, SSD_NGROUPS, -1)
    t = t * lax.rsqrt(jnp.mean(jnp.square(t), -1, keepdims=True) + RMS_EPS)
    return t.reshape(b, l, GROUP_W) * p["ssd_norm_w"].astype(F32), s_f, s_b


def token_mixing(h, hc, p, with_ctx):
    u = h @ p["w_in"]
    uc = hc @ p["w_in"]
    bc = hc.shape[0]
    z_s5 = jnp.zeros((bc, S5_GROUPS, S5_STATE), jnp.complex64)
    z_ssd = jnp.zeros((bc, SSD_NGROUPS, SSD_HEADS // SSD_NGROUPS, SSD_HEAD_DIM, SSD_STATE), F32)
    ys_c, s5_f, s5_b = s5_mix(uc[..., OFF_S5:OFF_S5 + GROUP_W], p, z_s5, z_s5)
    yd_c, sd_f, sd_b = ssd_mix(uc[..., OFF_SSD:], p, z_ssd, z_ssd)
    ys, _, _ = s5_mix(u[..., OFF_S5:OFF_S5 + GROUP_W], p, s5_f, s5_b)
    yd, _, _ = ssd_mix(u[..., OFF_SSD:], p, sd_f, sd_b)
    ya = fourier_mix(u[..., OFF_A:OFF_A + GROUP_W])
    yc = conv_module(u[..., OFF_CV:OFF_SSD], p)
    out = jnp.concatenate([ya, ys, yc, yd], -1).astype(h.dtype) @ p["w_out"]
    if not with_ctx:
        return out, None
    ya_c = fourier_mix(uc[..., OFF_A:OFF_A + GROUP_W])
    yc_c = conv_module(uc[..., OFF_CV:OFF_SSD], p)
    out_c = jnp.concatenate([ya_c, ys_c, yc_c, yd_c], -1).astype(hc.dtype) @ p["w_out"]
    return out, out_c


def swiglu(h, p):
    return (jax.nn.silu(h @ p["w_gate"]) * (h @ p["w_up"])) @ p["w_down"]


def setup_inputs(seed: int = 0) -> dict:
    key = jax.random.key(seed)
    ks = iter(jax.random.split(key, 48))
    L, D, G = DEPTH, D_MODEL, GROUP_W

    def nrm(shape, scale):
        return jax.random.normal(next(ks), shape, F32) * scale

    def uni(shape, lo, hi):
        return jax.random.uniform(next(ks), shape, F32, lo, hi)

    n_idx = jnp.arange(S5_STATE, dtype=F32)
    dt0 = jnp.exp(uni((L, 2, SSD_HEADS), math.log(1e-3), math.log(1e-1)))
    return {
        "x": nrm((BATCH, SEQ, D), 1.0),
        "c": nrm((BATCH, D), 1.0),
        "ctx": nrm((BATCH, CTX_LEN, D), 1.0),
        "c_ctx": nrm((D,), 1.0),
        "w_ada": nrm((L, D, 6 * D), 0.5 * D ** -0.5),
        "b_ada": nrm((L, 6 * D), 0.02),
        "w_in": nrm((L, D, IN_COLS), D ** -0.5),
        "s5_lam_re": -0.5 + nrm((L, 2, S5_GROUPS, S5_STATE), 0.01),
        "s5_lam_im": math.pi * n_idx + nrm((L, 2, S5_GROUPS, S5_STATE), 0.01),
        "s5_log_dt": uni((L, 2, S5_GROUPS), math.log(1e-3), math.log(1e-1)),
        "s5_b_re": nrm((L, S5_GROUPS, S5_STATE, S5_GROUP_CH), (2 * S5_GROUP_CH) ** -0.5),
        "s5_b_im": nrm((L, S5_GROUPS, S5_STATE, S5_GROUP_CH), (2 * S5_GROUP_CH) ** -0.5),
        "s5_c_re": nrm((L, S5_GROUPS, S5_GROUP_CH, S5_STATE), S5_STATE ** -0.5),
        "s5_c_im": nrm((L, S5_GROUPS, S5_GROUP_CH, S5_STATE), S5_STATE ** -0.5),
        "s5_d": nrm((L, G), 1.0),
        "s5_w_glu": nrm((L, G, G), G ** -0.5),
        "cv_w": nrm((L, CONV_WIDTH, G), CONV_WIDTH ** -0.5),
        "cv_b": nrm((L, G), 0.02),
        "cv_ln_g": 1.0 + nrm((L, G), 0.02),
        "cv_ln_b": nrm((L, G), 0.02),
        "ssd_conv_w": nrm((L, SSD_CONV, SSD_XBC), SSD_CONV ** -0.5),
        "ssd_conv_b": nrm((L, SSD_XBC), 0.02),
        "ssd_a_log": jnp.log(uni((L, 2, SSD_HEADS), 1.0, 16.0)),
        "ssd_dt_bias": dt0 + jnp.log(-jnp.expm1(-dt0)),
        "ssd_d": 1.0 + nrm((L, SSD_HEADS), 0.1),
        "ssd_norm_w": 1.0 + nrm((L, G), 0.02),
        "w_out": nrm((L, MIX_W, D), BETA * MIX_W ** -0.5),
        "ln1_g": 1.0 + nrm((L, D), 0.02),
        "ln1_b": nrm((L, D), 0.02),
        "w_gate": nrm((L, D, FFN_HIDDEN), D ** -0.5),
        "w_up": nrm((L, D, FFN_HIDDEN), D ** -0.5),
        "w_down": nrm((L, FFN_HIDDEN, D), BETA * FFN_HIDDEN ** -0.5),
        "ln2_g": 1.0 + nrm((L, D), 0.02),
        "ln2_b": nrm((L, D), 0.02),
    }


def reference(x, c, ctx, c_ctx, w_ada, b_ada, w_in, s5_lam_re, s5_lam_im, s5_log_dt,
              s5_b_re, s5_b_im, s5_c_re, s5_c_im, s5_d, s5_w_glu, cv_w, cv_b, cv_ln_g,
              cv_ln_b, ssd_conv_w, ssd_conv_b, ssd_a_log, ssd_dt_bias, ssd_d, ssd_norm_w,
              w_out, ln1_g, ln1_b, w_gate, w_up, w_down, ln2_g, ln2_b):
    bsz, seq, dm = x.shape
    rows = seq // GRID_W
    h = x + pos_2d(rows, dm).astype(x.dtype)[None]
    hc = ctx
    for i in range(DEPTH):
        with_ctx = i < DEPTH - 1
        p = {"w_in": w_in[i], "w_out": w_out[i],
             "s5_lam_re": s5_lam_re[i], "s5_lam_im": s5_lam_im[i], "s5_log_dt": s5_log_dt[i],
             "s5_b_re": s5_b_re[i], "s5_b_im": s5_b_im[i], "s5_c_re": s5_c_re[i],
             "s5_c_im": s5_c_im[i], "s5_d": s5_d[i], "s5_w_glu": s5_w_glu[i],
             "cv_w": cv_w[i], "cv_b": cv_b[i], "cv_ln_g": cv_ln_g[i], "cv_ln_b": cv_ln_b[i],
             "ssd_conv_w": ssd_conv_w[i], "ssd_conv_b": ssd_conv_b[i], "ssd_a_log": ssd_a_log[i],
             "ssd_dt_bias": ssd_dt_bias[i], "ssd_d": ssd_d[i], "ssd_norm_w": ssd_norm_w[i],
             "w_gate": w_gate[i], "w_up": w_up[i], "w_down": w_down[i]}
        m = jax.nn.silu(c) @ w_ada[i] + b_ada[i]
        sh1, sc1, g1, sh2, sc2, g2 = jnp.split(m[:, None, :], 6, axis=-1)
        mc = jax.nn.silu(c_ctx) @ w_ada[i] + b_ada[i]
        csh1, csc1, cg1, csh2, csc2, cg2 = jnp.split(mc, 6)
        mix, mix_c = token_mixing(h * (1.0 + sc1) + sh1, hc * (1.0 + csc1) + csh1, p, with_ctx)
        h = layer_norm(ALPHA * h + g1 * mix, ln1_g[i], ln1_b[i])
        h = layer_norm(ALPHA * h + g2 * swiglu(h * (1.0 + sc2) + sh2, p), ln2_g[i], ln2_b[i])
        if with_ctx:
            hc = layer_norm(ALPHA * hc + cg1 * mix_c, ln1_g[i], ln1_b[i])
            hc = layer_norm(ALPHA * hc + cg2 * swiglu(hc * (1.0 + csc2) + csh2, p), ln2_g[i], ln2_b[i])
    return h.astype(x.dtype)
```

```python
import math
import os
from contextlib import ExitStack
import numpy as np
import concourse.bass as bass
import concourse.mybir as mybir
from concourse.bass_utils import run_bass_kernel_spmd

F32 = mybir.dt.float32
BF16 = mybir.dt.bfloat16
I32 = mybir.dt.int32
U32 = mybir.dt.uint32
AF = mybir.ActivationFunctionType
ALU = mybir.AluOpType
PI = math.pi


class Cfg:
    def __init__(self, D=4096, S=8192, CT=256, FF=11008, depth=2):
        self.D, self.S, self.CT, self.FF, self.depth = D, S, CT, FF, depth
        self.GW = 1024
        self.QW = 256
        self.UQ = 1800
        self.SL, self.CL = S // 4, CT // 4
        self.TL = self.SL + self.CL
        self.LT = S + CT
        self.KD = D // 128
        self.KF = FF // 128
        self.ALPHA = (2 * depth) ** 0.25


class Buf:
    __slots__ = ("w", "rs", "name")

    def __init__(self, name=""):
        self.w = None
        self.rs = {}
        self.name = name


ENGS = ["pe", "act", "dve", "pool", "sp"]


class Sched:
    def __init__(self, nc, stack, n_dma=48):
        self.nc = nc
        self.sem = {e: stack.enter_context(nc.semaphore("s_" + e)) for e in ENGS}
        self.cnt = {e: 0 for e in ENGS}
        self.waited = {e: {} for e in ENGS}
        self.dsem = [stack.enter_context(nc.semaphore("d%d" % i)) for i in range(n_dma)]
        self.dcnt = [0] * n_dma
        self.dnext = 0
        self.csem = stack.enter_context(nc.semaphore("ccsem"))
        self.ccnt = 0
        self.prog = {e: [] for e in ENGS}
        self.ninst = 0

    def _h(self, key):
        if key[0] == "e":
            return self.sem[key[1]]
        if key[0] == "d":
            return self.dsem[key[1]]
        return self.csem

    def _wait(self, eng, key, val):
        if val <= 0:
            return
        if key == ("e", "pe") and eng == "pe":
            return
        if self.waited[eng].get(key, 0) < val:
            self.prog[eng].append(("wait", key, val))
            self.waited[eng][key] = val

    def _deps(self, eng, r, w):
        for b in r:
            if b.w is not None:
                self._wait(eng, *b.w)
        for b in w:
            if b.w is not None:
                self._wait(eng, *b.w)
            for k, v in b.rs.items():
                self._wait(eng, k, v)

    def _mark(self, ev, r, w):
        for b in r:
            if b.rs.get(ev[0], 0) < ev[1]:
                b.rs[ev[0]] = ev[1]
        for b in w:
            b.w = ev
            b.rs = {}

    def op(self, eng, fn, r=(), w=(), inc=True):
        assert inc or eng == "pe"
        self._deps(eng, r, w)
        if inc:
            self.cnt[eng] += 1
            ev = (("e", eng), self.cnt[eng])
        else:
            ev = (("e", eng), self.cnt[eng] + 1)
        self.prog[eng].append(("op", fn, inc))
        self._mark(ev, r, w)
        self.ninst += 1

    def dma(self, eng, out, in_, r=(), w=(), **kw):
        i = self.dnext
        self.dnext = (i + 1) % len(self.dsem)
        self._deps(eng, r, w)
        self._wait(eng, ("d", i), self.dcnt[i])
        self.dcnt[i] += 16
        ev = (("d", i), self.dcnt[i])
        self.prog[eng].append(("dma", (lambda e, o=out, n=in_, k=kw: e.dma_start(out=o, in_=n, **k)), i))
        self._mark(ev, r, w)
        self.ninst += 1

    def cc(self, kind, op, groups, in_ap, out_ap, r=(), w=()):
        eng = "pool"
        self._deps(eng, r, w)
        self._wait(eng, ("c", 0), self.ccnt)
        self.ccnt += 1
        ev = (("c", 0), self.ccnt)
        self.prog[eng].append(("cc", lambda e: e.collective_compute(
            kind, op, replica_groups=groups, ins=[in_ap], outs=[out_ap])))
        self._mark(ev, r, w)

    def barrier(self):
        for e in ENGS:
            for f in ENGS:
                if f != e:
                    self._wait(e, ("e", f), self.cnt[f])
            for i in range(len(self.dsem)):
                self._wait(e, ("d", i), self.dcnt[i])
            self._wait(e, ("c", 0), self.ccnt)

    def flush(self):
        nc = self.nc
        progs = self.prog
        self.prog = {e: [] for e in ENGS}

        def run(eng, e):
            for it in progs[eng]:
                if it[0] == "wait":
                    e.wait_ge(self._h(it[1]), it[2])
                elif it[0] == "op":
                    ins = it[1](e)
                    if it[2]:
                        ins.then_inc(self.sem[eng], 1)
                elif it[0] == "dma":
                    it[1](e).then_inc(self.dsem[it[2]], 16)
                else:
                    it[1](e).then_inc(self.csem)

        with nc.Block() as block:
            @block.tensor
            def _(e):
                run("pe", e)

            @block.scalar
            def _(e):
                run("act", e)

            @block.vector
            def _(e):
                run("dve", e)

            @block.gpsimd
            def _(e):
                run("pool", e)

            @block.sync
            def _(e):
                run("sp", e)


class K:
    def __init__(self, cfg, debug=()):
        self.cfg = cfg
        self.debug = set(debug)
        self.nc = bass.Bass("TRN2", target_bir_lowering=False)
        self.gstack = ExitStack()
        self.sc = Sched(self.nc, self.gstack)
        self.dram = {}
        self.dbuf = {}
        self.uid = 0

    def din(self, name, shape, dt=F32):
        t = self.nc.dram_tensor(name, list(shape), dt, kind="ExternalInput")
        self.dram[name] = t
        self.dbuf[name] = Buf(name)
        return t

    def dout(self, name, shape, dt=F32):
        t = self.nc.dram_tensor(name, list(shape), dt, kind="ExternalOutput")
        self.dram[name] = t
        self.dbuf[name] = Buf(name)
        return t

    def dscr(self, name, shape, dt=F32):
        t = self.nc.dram_tensor(name, list(shape), dt)
        self.dram[name] = t
        self.dbuf[name] = Buf(name)
        return t

    def sb(self, stack, shape, dt=F32, name=None):
        self.uid += 1
        t = stack.enter_context(self.nc.sbuf_tensor("%s_%d" % (name or "t", self.uid), list(shape), dt))
        return t

    def ps(self, stack, shape=(128, 512), dt=F32, name=None):
        self.uid += 1
        t = stack.enter_context(self.nc.psum_tensor("%s_%d" % (name or "p", self.uid), list(shape), dt))
        return t


def sbap(t, p0, pn, off, dims):
    full = t[p0:p0 + pn]
    base = full.offset
    pstep = full.ap[0][0]
    return bass.AP(t, base + off, [[pstep, pn]] + [list(d) for d in dims])


def OT(cfg, l, v, kc):
    return (l * 6 + v) * cfg.KD + kc


def make_ident(k):
    sc = k.sc
    st = k.gstack
    k.ident = k.sb(st, [128, 128], F32, "ident")
    k.identb = k.sb(st, [128, 128], BF16, "identb")
    k.ident_b = Buf("ident")
    with ExitStack() as tmp:
        ii = k.sb(tmp, [128, 128], I32)
        jj = k.sb(tmp, [128, 128], I32)
        b1, b2 = Buf(), Buf()
        sc.op("pool", lambda e: e.iota(ii[:], pattern=[[0, 128]], base=0, channel_multiplier=1), w=[b1])
        sc.op("pool", lambda e: e.iota(jj[:], pattern=[[1, 128]], base=0, channel_multiplier=0), w=[b2])
        sc.op("dve", lambda e: e.tensor_tensor(out=k.ident[:], in0=ii[:], in1=jj[:], op=ALU.is_equal),
              r=[b1, b2], w=[k.ident_b])
        sc.op("dve", lambda e: e.tensor_copy(out=k.identb[:], in_=k.ident[:]), r=[k.ident_b], w=[k.ident_b])
        sc.barrier()
        sc.flush()


def phase_adaln(k):
    cfg, sc, nc = k.cfg, k.sc, k.nc
    D, KD = cfg.D, cfg.KD
    KA = D // 8
    ksz = min(128, KA)
    nka = KA // ksz
    NOT = cfg.depth * 6 * KD
    npan = -(-6 * D // 2048)
    PW = 6 * D // npan
    assert PW * npan == 6 * D and PW % 128 == 0
    w_ada = k.din("w_ada_sh", [cfg.depth, KA, 6 * D])
    cT_d = k.din("cT_sh", [ksz, nka * 3])
    b_ada = k.din("b_ada_r", [NOT, 128])
    sel_d = k.din("sel", [128, 4])
    modp_d = k.dscr("modp", [128, NOT * 3])
    modall_d = k.dscr("modall", [8 * 128, NOT * 3])
    g = k.gstack
    k.MODO = k.sb(g, [128, NOT], F32, "MODO")
    k.MODC = k.sb(g, [128, NOT], F32, "MODC")
    k.ONEO = k.sb(g, [128, NOT], F32, "ONEO")
    k.ONEC = k.sb(g, [128, NOT], F32, "ONEC")
    k.SEL = k.sb(g, [128, 4], F32, "SEL")
    k.MODB = Buf("MOD")
    k.SELB = Buf("SEL")
    with ExitStack() as st:
        cT = k.sb(st, [ksz, nka, 3])
        pan = [k.sb(st, [ksz, nka, PW]) for _ in range(2)]
        panb = [Buf(), Buf()]
        modp = k.sb(st, [128, NOT, 3])
        pbank = [k.ps(st) for _ in range(2)]
        pbb = [Buf(), Buf()]
        bc, bmodp = Buf(), Buf()
        sc.dma("sp", cT[:].rearrange("p a j -> p (a j)"), cT_d.ap(), r=[k.dbuf["cT_sh"]], w=[bc])
        sc.op("act", lambda e: e.activation(out=cT[:], in_=cT[:], func=AF.Silu), r=[bc], w=[bc])
        sc.dma("sp", k.SEL[:], sel_d.ap(), r=[k.dbuf["sel"]], w=[k.SELB])
        ot = 0
        PB = 170
        pending = []
        for l in range(cfg.depth):
            for pn in range(npan):
                pi = (l * npan + pn) % 2
                src = w_ada.ap()[l].rearrange("(a p) n -> p a n", p=ksz)[:, :, pn * PW:(pn + 1) * PW]
                sc.dma("sp", pan[pi][:], src, r=[k.dbuf["w_ada_sh"]], w=[panb[pi]])
                for t in range(PW // 128):
                    bi = (ot // PB) % 2
                    col = (ot % PB) * 3
                    for a in range(nka):
                        sc.op("pe", lambda e, bi=bi, col=col, pi=pi, a=a, t=t: e.matmul(
                            pbank[bi][:, col:col + 3], lhsT=pan[pi][:, a, t * 128:(t + 1) * 128],
                            rhs=cT[:, a, :], start=(a == 0), stop=(a == nka - 1)),
                            r=[panb[pi], bc], w=[pbb[bi]], inc=(a == nka - 1))
                    ot += 1
                    if ot % PB == 0 or ot == NOT:
                        o0 = (ot - 1) // PB * PB
                        n = ot - o0
                        sc.op("dve", lambda e, bi=bi, o0=o0, n=n: e.tensor_copy(
                            out=modp[:, o0:o0 + n, :].rearrange("p a j -> p (a j)"), in_=pbank[bi][:, 0:3 * n]),
                            r=[pbb[bi]], w=[bmodp])
        sc.dma("sp", modp_d.ap(), modp[:].rearrange("p a j -> p (a j)"), r=[bmodp], w=[k.dbuf["modp"]])
        sc.cc("AllGather", ALU.bypass, [list(range(8))], modp_d.ap().opt(), modall_d.ap().opt(),
              r=[k.dbuf["modp"]], w=[k.dbuf["modall"]])
        allp = k.sb(st, [128, 8, NOT * 3])
        ball = Buf()
        sc.dma("sp", allp[:], modall_d.ap().rearrange("(r p) f -> p r f", p=128), r=[k.dbuf["modall"]], w=[ball])
        biasT = k.sb(st, [128, NOT])
        bbias = Buf()
        ngrp = (NOT + 127) // 128
        for gi in range(ngrp):
            rows = min(128, NOT - gi * 128)
            braw = k.sb(st, [128, 128])
            bb1 = Buf()
            sc.dma("sp", braw[0:rows, :], b_ada.ap()[gi * 128:gi * 128 + rows, :], r=[k.dbuf["b_ada_r"]], w=[bb1])
            sc.op("pe", lambda e, rows=rows, braw=braw: e.transpose(out=pbank[0][:, 0:rows], in_=braw[0:rows, :],
                                                                   identity=k.ident[0:rows, 0:rows]),
                  r=[bb1, k.ident_b], w=[pbb[0]])
            sc.op("dve", lambda e, gi=gi, rows=rows: e.tensor_copy(out=biasT[:, gi * 128:gi * 128 + rows],
                                                                 in_=pbank[0][:, 0:rows]), r=[pbb[0]], w=[bbias])
        tot = k.sb(st, [128, NOT, 3])
        btot = Buf()
        sc.op("dve", lambda e: e.tensor_tensor(out=tot[:].rearrange("p a j -> p (a j)"), in0=allp[:, 0, :],
                                               in1=allp[:, 1, :], op=ALU.add), r=[ball], w=[btot])
        for r_ in range(2, 8):
            sc.op("dve", lambda e, r_=r_: e.tensor_tensor(out=tot[:].rearrange("p a j -> p (a j)"),
                                                          in0=tot[:].rearrange("p a j -> p (a j)"),
                                                          in1=allp[:, r_, :], op=ALU.add), r=[ball, btot], w=[btot])
        sc.op("dve", lambda e: e.tensor_scalar(out=k.MODO[:], in0=tot[:, :, 0], scalar1=k.SEL[:, 0:1], scalar2=None,
                                               op0=ALU.mult), r=[btot, k.SELB], w=[k.MODB])
        sc.op("dve", lambda e: e.scalar_tensor_tensor(out=k.MODO[:], in0=tot[:, :, 1], scalar=k.SEL[:, 1:2],
                                                      in1=k.MODO[:], op0=ALU.mult, op1=ALU.add),
              r=[btot, k.SELB, k.MODB], w=[k.MODB])
        sc.op("dve", lambda e: e.tensor_tensor(out=k.MODO[:], in0=k.MODO[:], in1=biasT[:], op=ALU.add),
              r=[bbias, k.MODB], w=[k.MODB])
        sc.op("dve", lambda e: e.tensor_tensor(out=k.MODC[:], in0=tot[:, :, 2], in1=biasT[:], op=ALU.add),
              r=[bbias, btot], w=[k.MODB])
        sc.op("dve", lambda e: e.tensor_scalar(out=k.ONEO[:], in0=k.MODO[:], scalar1=1.0, scalar2=None, op0=ALU.add),
              r=[k.MODB], w=[k.MODB])
        sc.op("dve", lambda e: e.tensor_scalar(out=k.ONEC[:], in0=k.MODC[:], scalar1=1.0, scalar2=None, op0=ALU.add),
              r=[k.MODB], w=[k.MODB])
        if "mod" in k.debug:
            dd = k.dout("dbg_mod", [128, 2 * NOT])
            sc.dma("sp", dd.ap()[:, 0:NOT], k.MODO[:], r=[k.MODB], w=[k.dbuf["dbg_mod"]])
            sc.dma("sp", dd.ap()[:, NOT:2 * NOT], k.MODC[:], r=[k.MODB], w=[k.dbuf["dbg_mod"]])
        sc.barrier()
        sc.flush()


def phase_x(k):
    cfg, sc, nc = k.cfg, k.sc, k.nc
    D, KD, SL, CL, TL = cfg.D, cfg.KD, cfg.SL, cfg.CL, cfg.TL
    x_d = k.din("x_sh", [SL, D])
    c_d = k.din("ctx_sh", [CL, D])
    k.HT = k.dscr("HT_loc", [D, TL])
    HTv = k.HT.ap().rearrange("(c p) t -> p c t", p=128)
    q4 = KD // 4
    NR = SL // 64
    with ExitStack() as st:
        om = k.sb(st, [128, q4])
        idx = k.sb(st, [128, q4], I32)
        bom = Buf()
        sc.op("pool", lambda e: e.iota(idx[:], pattern=[[128, q4]], base=0, channel_multiplier=1), w=[bom])
        sc.op("act", lambda e: e.activation(out=om[:], in_=idx[:], func=AF.Exp, scale=-math.log(10000.0) / (D // 4)),
              r=[bom], w=[bom])
        rowv = k.sb(st, [128, NR])
        colv = k.sb(st, [128, 64])
        ri = k.sb(st, [128, NR], I32)
        ci = k.sb(st, [128, 64], I32)
        brc = Buf()
        sc.op("pool", lambda e: e.iota(ri[:], pattern=[[1, NR]], base=0, channel_multiplier=0), w=[brc])
        sc.op("pool", lambda e: e.iota(ci[:], pattern=[[1, 64]], base=0, channel_multiplier=0), w=[brc])
        sc.op("dve", lambda e: e.tensor_scalar(out=rowv[:], in0=ri[:], scalar1=k.SEL[:, 2:3], scalar2=None, op0=ALU.add),
              r=[brc, k.SELB], w=[brc])
        sc.op("dve", lambda e: e.tensor_copy(out=colv[:], in_=ci[:]), r=[brc], w=[brc])
        ROWT = k.sb(st, [128, 2 * q4, NR])
        COLT = k.sb(st, [128, 2 * q4, 64])
        btab = Buf()

        def trig(dst, src, n, kf, is_cos):
            s = k.sb(st, [128, n])
            si = k.sb(st, [128, n], I32)
            sf = k.sb(st, [128, n])
            ng = k.sb(st, [128, n])
            b = Buf()
            sc.op("dve", lambda e: e.tensor_scalar(out=s[:], in0=src, scalar1=om[:, kf:kf + 1], scalar2=1.0 / (2 * PI),
                                                   op0=ALU.mult, op1=ALU.mult), r=[brc, bom], w=[b])
            sc.op("dve", lambda e: e.tensor_scalar(out=s[:], in0=s[:], scalar1=(0.75 if is_cos else 0.5), scalar2=None,
                                                   op0=ALU.add), r=[b], w=[b])
            sc.op("dve", lambda e: e.tensor_copy(out=si[:], in_=s[:]), r=[b], w=[b])
            sc.op("dve", lambda e: e.tensor_copy(out=sf[:], in_=si[:]), r=[b], w=[b])
            sc.op("dve", lambda e: e.tensor_tensor(out=s[:], in0=s[:], in1=sf[:], op=ALU.subtract), r=[b], w=[b])
            sc.op("dve", lambda e: e.tensor_single_scalar(out=ng[:], in_=s[:], scalar=0.0, op=ALU.is_lt), r=[b], w=[b])
            sc.op("dve", lambda e: e.tensor_tensor(out=s[:], in0=s[:], in1=ng[:], op=ALU.add), r=[b], w=[b])
            sc.op("act", lambda e: e.activation(out=dst, in_=s[:], func=AF.Sin, scale=2 * PI, bias=-PI), r=[b], w=[btab])

        for kf in range(q4):
            trig(ROWT[:, kf, :], rowv[:], NR, kf, False)
            trig(ROWT[:, q4 + kf, :], rowv[:], NR, kf, True)
            trig(COLT[:, kf, :], colv[:], 64, kf, False)
            trig(COLT[:, q4 + kf, :], colv[:], 64, kf, True)
        GT = 4
        xt = [k.sb(st, [128, D]) for _ in range(2)]
        xtb = [Buf(), Buf()]
        hts = k.sb(st, [128, KD, GT * 128])
        htb = Buf()
        pb = [k.ps(st) for _ in range(4)]
        pbb = [Buf() for _ in range(4)]
        ntl = SL // 128
        tiles = [(t * 128, 128, True) for t in range(ntl)] + [(SL + c0, min(128, CL - c0), False) for c0 in range(0, CL, 128)]
        pcnt = 0
        gi = 0
        while gi < len(tiles):
            grp = tiles[gi:gi + GT]
            if not grp[0][2]:
                grp = grp[:1]
            for j, (t0, tn, lat) in enumerate(grp):
                xi = (gi + j) % 2
                src = x_d.ap()[t0:t0 + tn, :] if lat else c_d.ap()[t0 - SL:t0 - SL + tn, :]
                sc.dma("sp", xt[xi][0:tn, :], src, r=[k.dbuf["x_sh" if lat else "ctx_sh"]], w=[xtb[xi]])
                for kc in range(KD):
                    pi = pcnt % 4
                    pcnt += 1
                    sc.op("pe", lambda e, pi=pi, xi=xi, kc=kc, tn=tn: e.transpose(
                        out=pb[pi][:, 0:tn], in_=xt[xi][0:tn, kc * 128:(kc + 1) * 128], identity=k.ident[0:tn, 0:tn]),
                        r=[xtb[xi], k.ident_b], w=[pbb[pi]])
                    dst = hts[:, kc, j * 128:j * 128 + tn]
                    if lat:
                        typ, kf = kc // q4, kc % q4
                        r0 = t0 // 64
                        if typ < 2:
                            pos = sbap(ROWT, 0, 128, (typ * q4 + kf) * NR + r0, [[1, 2], [0, 64]])
                        else:
                            pos = sbap(COLT, 0, 128, ((typ - 2) * q4 + kf) * 64, [[0, 2], [1, 64]])
                        sc.op("dve", lambda e, dst=dst, pi=pi, pos=pos: e.tensor_tensor(
                            out=dst.rearrange("p (a b) -> p a b", b=64),
                            in0=pb[pi][:, 0:128].rearrange("p (a b) -> p a b", b=64), in1=pos, op=ALU.add),
                            r=[pbb[pi], btab], w=[htb])
                    else:
                        sc.op("act", lambda e, dst=dst, pi=pi, tn=tn: e.activation(out=dst, in_=pb[pi][:, 0:tn], func=AF.Copy),
                              r=[pbb[pi]], w=[htb])
            g0 = grp[0][0]
            gn = sum(t[1] for t in grp)
            sc.dma("sp", HTv[:, :, g0:g0 + gn], hts[:, :, 0:gn], r=[htb], w=[k.dbuf["HT_loc"]])
            gi += len(grp)
        sc.barrier()
        sc.flush()


def quarter_cols(q):
    G = 1024
    a = np.arange
    off_ssd = 4 * G
    xbc = off_ssd + G
    dt0 = off_ssd + G + 2048
    return np.concatenate([
        0 * G + 256 * q + a(256), 1 * G + 256 * q + a(256), 2 * G + 256 * q + a(256), 3 * G + 256 * q + a(256),
        off_ssd + 256 * q + a(256), xbc + 256 * q + a(256), xbc + G + 128 * q + a(128), xbc + G + 512 + 128 * q + a(128),
        dt0 + 4 * q + a(4), dt0 + 16 + 4 * q + a(4)])


def host_inputs(cfg, inp, need):
    D, S, CT = cfg.D, cfg.S, cfg.CT
    KD = cfg.KD
    KA = D // 8
    ksz = min(128, KA)
    nka = KA // ksz
    f = lambda a: np.ascontiguousarray(a, dtype=np.float32)
    maps = []
    cvec = np.concatenate([inp["c"], inp["c_ctx"][None]], 0)
    for c in range(8):
        b, q = c // 4, c % 4
        m = {}
        m["x_sh"] = f(inp["x"][b, q * cfg.SL:(q + 1) * cfg.SL])
        m["ctx_sh"] = f(inp["ctx"][b, q * cfg.CL:(q + 1) * cfg.CL])
        cs = cvec[:, c * KA:(c + 1) * KA]
        m["cT_sh"] = f(cs.reshape(3, nka, ksz).transpose(2, 1, 0).reshape(ksz, nka * 3))
        m["w_ada_sh"] = f(inp["w_ada"][:, c * KA:(c + 1) * KA, :])
        m["b_ada_r"] = f(inp["b_ada"].reshape(cfg.depth * 6 * KD, 128))
        sel = np.zeros((128, 4), np.float32)
        sel[:, b] = 1.0
        sel[:, 2] = q * cfg.SL // 64
        m["sel"] = sel
        if "w_in_q" in need:
            m["w_in_q"] = f(inp["w_in"][:, :, quarter_cols(q)])
        if "lngb" in need:
            lg = np.stack([inp["ln1_g"], inp["ln1_b"], inp["ln2_g"], inp["ln2_b"]], 1)
            m["lngb"] = f(lg.reshape(cfg.depth, 4, KD, 128).transpose(3, 0, 1, 2).reshape(128, -1))
        if "cvp" in need:
            cw = inp["cv_w"][:, :, 256 * q:256 * (q + 1)]
            parts = [cw.transpose(0, 2, 1), inp["cv_b"][:, 256 * q:256 * (q + 1), None], inp["cv_ln_g"][:, 256 * q:256 * (q + 1), None],
                     inp["cv_ln_b"][:, 256 * q:256 * (q + 1), None], np.zeros((cfg.depth, 256, 1), np.float32)]
            cv = np.concatenate(parts, 2)
            m["cvp"] = f(cv.reshape(cfg.depth, 2, 128, 35).transpose(0, 2, 1, 3).reshape(cfg.depth, 128, 70))
        if "w_out_q" in need:
            rows = np.concatenate([g_ * 1024 + 256 * q + np.arange(256) for g_ in range(4)])
            m["w_out_q"] = f(inp["w_out"][:, rows, :])
        if "ssdcw" in need:
            G = 1024
            ch = np.concatenate([256 * q + np.arange(256), G + 128 * q + np.arange(128), G + 512 + 128 * q + np.arange(128)])
            cw = np.concatenate([inp["ssd_conv_w"][:, :, ch].transpose(0, 2, 1), inp["ssd_conv_b"][:, ch, None]], 2)
            m["ssdcw"] = f(cw.reshape(cfg.depth, 4, 128, 6).transpose(0, 2, 1, 3).reshape(cfg.depth, 128, 24))
            hs = 4 * q + np.arange(4)
            rowv = np.concatenate([inp["ssd_dt_bias"][:, :, hs].reshape(cfg.depth, 8), inp["ssd_a_log"][:, :, hs].reshape(cfg.depth, 8),
                                   np.repeat(inp["ssd_d"][:, hs], 64, axis=1), inp["ssd_norm_w"][:, 256 * q:256 * (q + 1)]], 1)
            m["ssdrow"] = f(np.broadcast_to(rowv[:, None, :], (cfg.depth, 128, 528)))
        if "s5p" in need:
            gs = 16 * q + np.arange(16)
            dup = lambda a: np.concatenate([a, a], 0)
            cols = []
            for nm in ("s5_lam_re", "s5_lam_im"):
                a = inp[nm][:, :, gs, :]
                cols.append(np.stack([dup(a[l_].transpose(2, 0, 1).reshape(64, 32)) for l_ in range(cfg.depth)]))
            ld = inp["s5_log_dt"][:, :, gs].reshape(cfg.depth, 1, 32)
            cols.append(np.broadcast_to(ld, (cfg.depth, 128, 32)))
            for nm in ("s5_b_re", "s5_b_im"):
                a = inp[nm][:, gs]
                cols.append(np.stack([dup(a[l_].transpose(1, 0, 2).reshape(64, 256)) for l_ in range(cfg.depth)]))
            for nm in ("s5_c_re", "s5_c_im"):
                a = inp[nm][:, gs]
                cols.append(np.stack([dup(a[l_].transpose(2, 0, 1).reshape(64, 256)) for l_ in range(cfg.depth)]))
            m["s5p"] = f(np.concatenate(cols, 2))
            m["s5d"] = f(inp["s5_d"][:, 256 * q:256 * (q + 1)].reshape(cfg.depth, 16, 16).transpose(0, 2, 1))
            m["s5glu_q"] = f(inp["s5_w_glu"][:, 256 * q:256 * (q + 1), :])
        if "w_gate_sh" in need:
            Q4 = D // 4
            m["w_gate_sh"] = f(inp["w_gate"][:, q * Q4:(q + 1) * Q4, :])
            m["w_up_sh"] = f(inp["w_up"][:, q * Q4:(q + 1) * Q4, :])
            m["w_down_sh"] = f(inp["w_down"][:, :, q * Q4:(q + 1) * Q4])
        maps.append({k_: v for k_, v in m.items() if k_ in need})
    return maps


GROUPS4 = [[0, 1, 2, 3], [4, 5, 6, 7]]


def ag4_chunked(k, src, dst, rows, rc, sname, dname):
    assert rows % rc == 0
    for j in range(rows // rc):
        k.sc.cc("AllGather", ALU.bypass, GROUPS4, src.ap()[j * rc:(j + 1) * rc, :].opt(), dst.ap()[j * 4 * rc:(j + 1) * 4 * rc, :].opt(),
                r=[k.dbuf[sname]], w=[k.dbuf[dname]])


HT_RC = 64
WGU_RC = 512
WD_RC = 32


class CastLoader:
    def __init__(self, k, st, width):
        self.k = k
        self.t = [k.sb(st, [128, width]) for _ in range(2)]
        self.b = [Buf(), Buf()]
        self.n = 0
        self.width = width

    def load(self, dst, src, rows, n, rbuf, wbuf):
        sc = self.k.sc
        assert n <= self.width
        i = self.n % 2
        self.n += 1
        t, b = self.t[i], self.b[i]
        sc.dma("sp", t[0:rows, 0:n], src, r=[rbuf], w=[b])
        if self.n % 2:
            sc.op("dve", lambda e: e.tensor_copy(out=dst, in_=t[0:rows, 0:n]), r=[b], w=[wbuf])
        else:
            sc.op("act", lambda e: e.activation(out=dst, in_=t[0:rows, 0:n], func=AF.Copy), r=[b], w=[wbuf])


def tok_tiles(cfg, T):
    out = [(c0, min(T, cfg.SL - c0), True) for c0 in range(0, cfg.SL, T)]
    out += [(cfg.SL + c0, min(T, cfg.CL - c0), False) for c0 in range(0, cfg.CL, T)]
    return out


def seq_col(cfg, rb, c0, lat):
    return rb * cfg.SL + c0 if lat else cfg.S + rb * cfg.CL + (c0 - cfg.SL)


def modulate(k, dst, src, l, v_scale, v_shift, kc, lat, r, w, i):
    cfg, sc = k.cfg, k.sc
    one = k.ONEO if lat else k.ONEC
    mod = k.MODO if lat else k.MODC
    a = one[:, OT(cfg, l, v_scale, kc):OT(cfg, l, v_scale, kc) + 1]
    b = mod[:, OT(cfg, l, v_shift, kc):OT(cfg, l, v_shift, kc) + 1]
    eng = ("dve", "pool", "act")[i % 3]
    if eng == "act":
        sc.op("act", lambda e: e.activation(out=dst, in_=src, func=AF.Identity, scale=a, bias=b), r=r + [k.MODB], w=w)
    else:
        sc.op(eng, lambda e: e.tensor_scalar(out=dst, in0=src, scalar1=a, scalar2=b, op0=ALU.mult, op1=ALU.add),
              r=r + [k.MODB], w=w)


def phase_A(k, l):
    cfg, sc = k.cfg, k.sc
    D, KD, TL, UQ = cfg.D, cfg.KD, cfg.TL, cfg.UQ
    if l == 0:
        k.w_in = k.din("w_in_q", [cfg.depth, D, UQ])
        k.HTA = k.dscr("HT_all", [4 * D, TL])
        k.UT = k.dscr("UT", [UQ, cfg.LT])
    ag4_chunked(k, k.HT, k.HTA, D, HT_RC, "HT_loc", "HT_all")
    T = 256
    with ExitStack() as st:
        W = k.sb(st, [128, KD, UQ], BF16)
        bW = Buf()
        cl = CastLoader(k, st, UQ)
        for kc in range(KD):
            cl.load(W[:, kc, :], k.w_in.ap()[l, kc * 128:(kc + 1) * 128, :], 128, UQ, k.dbuf["w_in_q"], bW)
        ht = k.sb(st, [128, KD, T])
        bht = Buf()
        A = [k.sb(st, [128, KD, T], BF16) for _ in range(2)]
        bA = [Buf(), Buf()]
        uo = [k.sb(st, [128, T]) for _ in range(2)]
        buo = [Buf(), Buf()]
        pb = [k.ps(st) for _ in range(4)]
        pbb = [Buf() for _ in range(4)]
        ti = 0
        oc = 0
        otiles = [(r0, min(128, UQ - r0)) for r0 in range(0, UQ, 128)]
        for rb in range(4):
            for (c0, tn, lat) in tok_tiles(cfg, T):
                nh = 128 // HT_RC
                for kc in range(KD):
                    for hf in range(nh):
                        r0_ = ((kc * nh + hf) * 4 + rb) * HT_RC
                        sc.dma("sp", ht[hf * HT_RC:(hf + 1) * HT_RC, kc, 0:tn], k.HTA.ap()[r0_:r0_ + HT_RC, c0:c0 + tn], r=[k.dbuf["HT_all"]], w=[bht])
                ai = ti % 2
                ti += 1
                for kc in range(KD):
                    modulate(k, A[ai][:, kc, 0:tn], ht[:, kc, 0:tn], l, 1, 0, kc, lat, [bht], [bA[ai]], kc)
                col = seq_col(cfg, rb, c0, lat)
                for (r0, rn) in otiles:
                    pi = oc % 4
                    ui = oc % 2
                    oc += 1
                    for kc in range(KD):
                        sc.op("pe", lambda e, pi=pi, rn=rn, tn=tn, kc=kc, r0=r0, ai=ai: e.matmul(
                            pb[pi][0:rn, 0:tn], lhsT=W[:, kc, r0:r0 + rn], rhs=A[ai][:, kc, 0:tn],
                            start=(kc == 0), stop=(kc == KD - 1)), r=[bW, bA[ai]], w=[pbb[pi]], inc=(kc == KD - 1))
                    sc.op("act", lambda e, pi=pi, ui=ui, rn=rn, tn=tn: e.activation(
                        out=uo[ui][0:rn, 0:tn], in_=pb[pi][0:rn, 0:tn], func=AF.Copy), r=[pbb[pi]], w=[buo[ui]])
                    sc.dma("sp", k.UT.ap()[r0:r0 + rn, col:col + tn], uo[ui][0:rn, 0:tn], r=[buo[ui]], w=[k.dbuf["UT"]])
        sc.barrier()
        sc.flush()


def ln_pass(k, l, which, fill, dst, dst_name, extra=None):
    cfg, sc = k.cfg, k.sc
    D, KD = cfg.D, cfg.KD
    T = 512
    gi = lambda kc: ((l * 4 + 2 * which) * KD + kc)
    bi = lambda kc: ((l * 4 + 2 * which + 1) * KD + kc)
    dv = dst.ap().rearrange("(c p) t -> p c t", p=128)
    with ExitStack() as st:
        X = k.sb(st, [128, KD, T])
        bX = Buf()
        ones = k.sb(st, [128, 128])
        bo = Buf()
        sc.op("dve", lambda e: e.memset(ones[:], 1.0 / D), w=[bo])
        sq = [k.sb(st, [128, T]) for _ in range(2)]
        bsq = [Buf(), Buf()]
        mean = k.sb(st, [128, T])
        rstd = k.sb(st, [128, T])
        bm, br = Buf(), Buf()
        pm, pv = k.ps(st), k.ps(st)
        bpm, bpv = Buf(), Buf()
        for (c0, tn, lat) in tok_tiles(cfg, T):
            fill(X, c0, tn, lat, bX)
            for kc in range(KD):
                sc.op("pe", lambda e, kc=kc, tn=tn: e.matmul(pm[:, 0:tn], lhsT=ones[:], rhs=X[:, kc, 0:tn],
                                                             start=(kc == 0), stop=(kc == KD - 1)),
                      r=[bo, bX], w=[bpm], inc=(kc == KD - 1))
            sc.op("act", lambda e, tn=tn: e.activation(out=mean[:, 0:tn], in_=pm[:, 0:tn], func=AF.Copy), r=[bpm], w=[bm])
            for kc in range(KD):
                eng = ("dve", "pool")[kc % 2]
                sc.op(eng, lambda e, kc=kc, tn=tn: e.tensor_tensor(out=X[:, kc, 0:tn], in0=X[:, kc, 0:tn],
                                                                   in1=mean[:, 0:tn], op=ALU.subtract),
                      r=[bm, bX], w=[bX])
            for kc in range(KD):
                si = kc % 2
                sc.op("act", lambda e, kc=kc, tn=tn, si=si: e.activation(out=sq[si][:, 0:tn], in_=X[:, kc, 0:tn],
                                                                          func=AF.Square), r=[bX], w=[bsq[si]])
                sc.op("pe", lambda e, kc=kc, tn=tn, si=si: e.matmul(pv[:, 0:tn], lhsT=ones[:], rhs=sq[si][:, 0:tn],
                                                                    start=(kc == 0), stop=(kc == KD - 1)),
                      r=[bo, bsq[si]], w=[bpv], inc=True)
            sc.op("act", lambda e, tn=tn: e.activation(out=rstd[:, 0:tn], in_=pv[:, 0:tn], func=AF.Sqrt, bias=1e-5),
                  r=[bpv], w=[br])
            sc.op("dve", lambda e, tn=tn: e.reciprocal(out=rstd[:, 0:tn], in_=rstd[:, 0:tn]), r=[br], w=[br])
            for kc in range(KD):
                eng = ("dve", "pool")[kc % 2]
                sc.op(eng, lambda e, kc=kc, tn=tn: e.tensor_tensor(out=X[:, kc, 0:tn], in0=X[:, kc, 0:tn],
                                                                   in1=rstd[:, 0:tn], op=ALU.mult), r=[br, bX], w=[bX])
                g_ = k.LNGB[:, gi(kc):gi(kc) + 1]
                b_ = k.LNGB[:, bi(kc):bi(kc) + 1]
                sc.op("act", lambda e, kc=kc, tn=tn, g_=g_, b_=b_: e.activation(
                    out=X[:, kc, 0:tn], in_=X[:, kc, 0:tn], func=AF.Identity, scale=g_, bias=b_),
                    r=[bX, k.LNGBB], w=[bX])
            sc.dma("sp", dv[:, :, c0:c0 + tn], X[:, :, 0:tn], r=[bX], w=[k.dbuf[dst_name]])
            if extra is not None:
                extra(X, c0, tn, lat, bX)
        sc.barrier()
        sc.flush()


def load_small(k):
    cfg, sc = k.cfg, k.sc
    n = cfg.depth * 4 * cfg.KD
    d = k.din("lngb", [128, n])
    k.LNGB = k.sb(k.gstack, [128, n], F32, "LNGB")
    k.LNGBB = Buf()
    sc.dma("sp", k.LNGB[:], d.ap(), r=[k.dbuf["lngb"]], w=[k.LNGBB])


def phase_C1(k, l):
    cfg, sc = k.cfg, k.sc
    D, KD, TL = cfg.D, cfg.KD, cfg.TL
    if l == 0:
        k.H1 = k.dscr("H1T", [D, TL])
    hv = k.HT.ap().rearrange("(c p) t -> p c t", p=128)
    mv = k.MIX.ap().rearrange("(c p) t -> p c t", p=128)
    with ExitStack() as st:
        M = k.sb(st, [128, KD, 512])
        bM = Buf()

        def fill(X, c0, tn, lat, bX):
            sc.dma("sp", X[:, :, 0:tn], hv[:, :, c0:c0 + tn], r=[k.dbuf["HT_loc"]], w=[bX])
            sc.dma("sp", M[:, :, 0:tn], mv[:, :, c0:c0 + tn], r=[k.dbuf["MIX_loc"]], w=[bM])
            mod = k.MODO if lat else k.MODC
            for kc in range(KD):
                g1 = mod[:, OT(cfg, l, 2, kc):OT(cfg, l, 2, kc) + 1]
                sc.op("pool", lambda e, kc=kc, tn=tn, g1=g1: e.tensor_scalar(
                    out=M[:, kc, 0:tn], in0=M[:, kc, 0:tn], scalar1=g1, scalar2=None, op0=ALU.mult),
                    r=[bM, k.MODB], w=[bM])
                sc.op("dve", lambda e, kc=kc, tn=tn: e.scalar_tensor_tensor(
                    out=X[:, kc, 0:tn], in0=X[:, kc, 0:tn], scalar=cfg.ALPHA, in1=M[:, kc, 0:tn],
                    op0=ALU.mult, op1=ALU.add), r=[bM, bX], w=[bX])

        ln_pass(k, l, 0, fill, k.H1, "H1T")


def phase_G(k):
    cfg, sc = k.cfg, k.sc
    D, FF, KD, KF, dep = cfg.D, cfg.FF, cfg.KD, cfg.KF, cfg.depth
    kcr = KD // 4
    GP = 128
    npan = FF // GP
    wg = k.din("w_gate_sh", [dep, D // 4, FF])
    wu = k.din("w_up_sh", [dep, D // 4, FF])
    wd = k.din("w_down_sh", [dep, FF, D // 4])
    R0 = 2 * npan * 128
    R1 = dep * kcr * 128
    k.WGU_loc = [k.dscr("WGU_loc%d" % l_, [R0, kcr * GP], BF16) for l_ in range(dep)]
    k.WGU = [k.dscr("WGU_all%d" % l_, [4 * R0, kcr * GP], BF16) for l_ in range(dep)]
    k.WD_loc = k.dscr("WD_loc", [R1, KF * 128], BF16)
    k.WD = k.dscr("WD_all", [4 * R1, KF * 128], BF16)
    k.ffn_geom = (kcr, GP, npan, R0, R1)
    CW = min(FF, 2048)
    with ExitStack() as st:
        stg = [k.sb(st, [128, CW]) for _ in range(2)]
        bst = [Buf(), Buf()]
        cb = [k.sb(st, [128, CW], BF16) for _ in range(2)]
        bcb = [Buf(), Buf()]
        it = 0
        for l in range(dep):
            for gu, wsrc, nm in ((0, wg, "w_gate_sh"), (1, wu, "w_up_sh")):
                for kc in range(kcr):
                    for c0 in range(0, FF, CW):
                        cn = min(CW, FF - c0)
                        i = it % 2
                        it += 1
                        sc.dma("sp", stg[i][:, 0:cn], wsrc.ap()[l, kc * 128:(kc + 1) * 128, c0:c0 + cn], r=[k.dbuf[nm]], w=[bst[i]])
                        eng = ("dve", "act")[it % 2]
                        if eng == "act":
                            sc.op("act", lambda e, i=i, cn=cn: e.activation(out=cb[i][:, 0:cn], in_=stg[i][:, 0:cn], func=AF.Copy),
                                  r=[bst[i]], w=[bcb[i]])
                        else:
                            sc.op(eng, lambda e, i=i, cn=cn: e.tensor_copy(out=cb[i][:, 0:cn], in_=stg[i][:, 0:cn]),
                                  r=[bst[i]], w=[bcb[i]])
                        p0 = c0 // GP
                        pn = cn // GP
                        row0 = (gu * npan + p0) * 128
                        dstv = k.WGU_loc[l].ap()[row0:row0 + pn * 128, kc * GP:(kc + 1) * GP].rearrange("(a p) c -> p a c", p=128)
                        sc.dma("sp", dstv, cb[i][:, 0:cn].rearrange("p (a c) -> p a c", c=GP), r=[bcb[i]], w=[k.dbuf["WGU_loc%d" % l]])
            HG = max(1, min(KF, CW // (kcr * 128)))
            for h0 in range(0, KF, HG):
                hn = min(HG, KF - h0)
                i = it % 2
                it += 1
                w_ = kcr * 128
                srcv = wd.ap()[l, h0 * 128:(h0 + hn) * 128, :].rearrange("(h p) c -> p h c", p=128)
                sc.dma("sp", stg[i][:, 0:hn * w_].rearrange("p (h c) -> p h c", c=w_), srcv, r=[k.dbuf["w_down_sh"]], w=[bst[i]])
                sc.op("dve", lambda e, i=i, n=hn * w_: e.tensor_copy(out=cb[i][:, 0:n], in_=stg[i][:, 0:n]), r=[bst[i]], w=[bcb[i]])
                for dtl in range(kcr):
                    row0 = (l * kcr + dtl) * 128
                    dstv = k.WD_loc.ap()[row0:row0 + 128, h0 * 128:(h0 + hn) * 128].rearrange("p (h c) -> p h c", c=128)
                    srcs = cb[i][:, 0:hn * w_].rearrange("p (h c) -> p h c", c=w_)[:, :, dtl * 128:(dtl + 1) * 128]
                    sc.dma("sp", dstv, srcs, r=[bcb[i]], w=[k.dbuf["WD_loc"]])
        for l_ in range(dep):
            ag4_chunked(k, k.WGU_loc[l_], k.WGU[l_], R0, WGU_RC, "WGU_loc%d" % l_, "WGU_all%d" % l_)
        ag4_chunked(k, k.WD_loc, k.WD, R1, WD_RC, "WD_loc", "WD_all")
        sc.barrier()
        sc.flush()


def phase_FFN(k, l):
    cfg, sc = k.cfg, k.sc
    D, FF, KD, KF, TL = cfg.D, cfg.FF, cfg.KD, cfg.KF, cfg.TL
    kcr, GP, npan, R0, R1 = k.ffn_geom
    if l == 0:
        k.PRE2 = k.dscr("PRE2", [D, TL])
    T = 512
    h1v = k.H1.ap().rearrange("(c p) t -> p c t", p=128)
    with ExitStack() as st:
        A = k.sb(st, [128, KD, T], BF16)
        bA = Buf()
        ACTT = k.sb(st, [128, KF, T], BF16)
        bACT = Buf()
        SG = 4
        stg = [k.sb(st, [128, SG, T]) for _ in range(2)]
        bstg = [Buf(), Buf()]
        gsz, dsz = KD * GP, KF * 128
        reg = k.sb(st, [128, max(4 * gsz, 2 * dsz)], BF16)
        gs = [reg[:, i * gsz:(i + 1) * gsz].rearrange("p (r k c) -> p r (k c)", r=4, k=kcr) for i in range(4)]
        gsm = [reg[:, i * gsz:(i + 1) * gsz].rearrange("p (k c) -> p k c", c=GP) for i in range(4)]
        ds = [reg[:, i * dsz:(i + 1) * dsz] for i in range(2)]
        dsm = [reg[:, i * dsz:(i + 1) * dsz].rearrange("p (h c) -> p h c", c=128) for i in range(2)]
        bg = [Buf() for _ in range(4)]
        bd = [Buf() for _ in range(2)]
        sil = [k.sb(st, [128, T]) for _ in range(2)]
        bsil = [Buf(), Buf()]
        h1t = [k.sb(st, [128, T]) for _ in range(2)]
        bh1 = [Buf(), Buf()]
        ot = [k.sb(st, [128, T]) for _ in range(2)]
        bot = [Buf(), Buf()]
        pg = [k.ps(st) for _ in range(2)]
        pu = [k.ps(st) for _ in range(2)]
        pd = [k.ps(st) for _ in range(2)]
        bpg, bpu, bpd = [Buf(), Buf()], [Buf(), Buf()], [Buf(), Buf()]
        gcnt = 0
        dcnt = 0
        for (c0, tn, lat) in tok_tiles(cfg, T):
            for s0 in range(0, KD, SG):
                si = (s0 // SG) % 2
                sc.dma("sp", stg[si][:, :, 0:tn], h1v[:, s0:s0 + SG, c0:c0 + tn], r=[k.dbuf["H1T"]], w=[bstg[si]])
                for j in range(SG):
                    modulate(k, A[:, s0 + j, 0:tn], stg[si][:, j, 0:tn], l, 4, 3, s0 + j, lat, [bstg[si]], [bA], j)
            for hc in range(KF):
                slots = []
                for gu in range(2):
                    i = gcnt % 4
                    gcnt += 1
                    row0 = (gu * npan + hc) * 128
                    cw_ = kcr * GP
                    j_, i0_ = row0 // WGU_RC, row0 % WGU_RC
                    srcv = bass.AP(k.WGU[l], (j_ * 4 * WGU_RC + i0_) * cw_, [[cw_, 128], [WGU_RC * cw_, 4], [1, cw_]])
                    sc.dma("sp", gs[i], srcv, r=[k.dbuf["WGU_all%d" % l]], w=[bg[i]] + bd)
                    slots.append(i)
                pi = hc % 2
                for gu, (pp, bpp) in enumerate(((pg, bpg), (pu, bpu))):
                    i = slots[gu]
                    for kc in range(KD):
                        sc.op("pe", lambda e, pp=pp, pi=pi, i=i, kc=kc, tn=tn: e.matmul(
                            pp[pi][:, 0:tn], lhsT=gsm[i][:, kc, :], rhs=A[:, kc, 0:tn], start=(kc == 0), stop=(kc == KD - 1)),
                            r=[bg[i], bA], w=[bpp[pi]], inc=(kc == KD - 1))
                sc.op("act", lambda e, pi=pi, tn=tn: e.activation(out=sil[pi][:, 0:tn], in_=pg[pi][:, 0:tn], func=AF.Silu),
                      r=[bpg[pi]], w=[bsil[pi]])
                sc.op("dve", lambda e, pi=pi, tn=tn, hc=hc: e.tensor_tensor(out=ACTT[:, hc, 0:tn], in0=sil[pi][:, 0:tn],
                                                                           in1=pu[pi][:, 0:tn], op=ALU.mult),
                      r=[bsil[pi], bpu[pi]], w=[bACT])
            mod = k.MODO if lat else k.MODC
            for dt in range(KD):
                i = dcnt % 2
                dcnt += 1
                r_, dtl = dt // kcr, dt % kcr
                lr0 = (l * kcr + dtl) * 128
                for p0 in range(0, 128, WD_RC):
                    j_ = (lr0 + p0) // WD_RC
                    row0 = (j_ * 4 + r_) * WD_RC
                    sc.dma("sp", ds[i][p0:p0 + WD_RC, :], k.WD.ap()[row0:row0 + WD_RC, :], r=[k.dbuf["WD_all"]], w=[bd[i]] + bg)
                sc.dma("sp", h1t[i][:, 0:tn], k.H1.ap()[dt * 128:(dt + 1) * 128, c0:c0 + tn], r=[k.dbuf["H1T"]], w=[bh1[i]])
                for hc in range(KF):
                    sc.op("pe", lambda e, i=i, hc=hc, tn=tn: e.matmul(pd[i][:, 0:tn], lhsT=dsm[i][:, hc, :], rhs=ACTT[:, hc, 0:tn],
                                                                      start=(hc == 0), stop=(hc == KF - 1)),
                          r=[bd[i], bACT], w=[bpd[i]], inc=(hc == KF - 1))
                g2 = mod[:, OT(cfg, l, 5, dt):OT(cfg, l, 5, dt) + 1]
                sc.op("act", lambda e, i=i, tn=tn, g2=g2: e.activation(out=ot[i][:, 0:tn], in_=pd[i][:, 0:tn], func=AF.Identity, scale=g2),
                      r=[bpd[i], k.MODB], w=[bot[i]])
                sc.op("dve", lambda e, i=i, tn=tn: e.scalar_tensor_tensor(out=ot[i][:, 0:tn], in0=h1t[i][:, 0:tn], scalar=cfg.ALPHA,
                                                                          in1=ot[i][:, 0:tn], op0=ALU.mult, op1=ALU.add),
                      r=[bh1[i], bot[i]], w=[bot[i]])
                sc.dma("sp", k.PRE2.ap()[dt * 128:(dt + 1) * 128, c0:c0 + tn], ot[i][:, 0:tn], r=[bot[i]], w=[k.dbuf["PRE2"]])
        sc.barrier()
        sc.flush()


def phase_C3(k, l):
    cfg, sc = k.cfg, k.sc
    pv = k.PRE2.ap().rearrange("(c p) t -> p c t", p=128)

    def fill(X, c0, tn, lat, bX):
        sc.dma("sp", X[:, :, 0:tn], pv[:, :, c0:c0 + tn], r=[k.dbuf["PRE2"]], w=[bX])

    ln_pass(k, l, 1, fill, k.HT, "HT_loc")


U_FN, U_S5, U_CV, U_CG, U_Z, U_X, U_B, U_C, U_DTF, U_DTB = 0, 256, 512, 768, 1024, 1280, 1536, 1664, 1792, 1796
EPS = 1e-5
import os
SSD_LIM = int(os.environ.get('SSD_LIM', '99'))


def mixer_setup(k):
    cfg = k.cfg
    k.YT = k.dscr("YT", [1024, cfg.LT])
    k.ZT = k.dscr("ZT", [1024, cfg.LT], BF16)
    k.CST_loc = k.dscr("CST_loc", [2, cfg.LT])
    k.CST = k.dscr("CST_all", [8, cfg.LT])
    k.MIXP = k.dscr("MIXP", [4 * cfg.D, cfg.TL])
    k.MIX = k.dscr("MIX_loc", [cfg.D, cfg.TL])
    k.cvp = k.din("cvp", [cfg.depth, 128, 2 * 35])
    k.w_out = k.din("w_out_q", [cfg.depth, 1024, cfg.D])
    k.seqs = [(0, cfg.S), (cfg.S, cfg.CT)]


def phase_conv(k, l):
    cfg, sc = k.cfg, k.sc
    LT = cfg.LT
    T = 512
    with ExitStack() as st:
        cvp = k.sb(st, [128, 2, 35])
        bcv = Buf()
        sc.dma("sp", cvp[:].rearrange("p a b -> p (a b)"), k.cvp.ap()[l], r=[k.dbuf["cvp"]], w=[bcv])
        acc = k.sb(st, [128, 2, LT])
        bacc = Buf()
        with ExitStack() as s2:
            Lm = cfg.S
            hp = k.sb(s2, [128, Lm + 30])
            gt = k.sb(s2, [128, Lm])
            bh, bg = Buf(), Buf()
            for ti in range(2):
                for (c0, L) in k.seqs:
                    sc.dma("sp", hp[:, 15:15 + L], k.UT.ap()[U_CV + ti * 128:U_CV + (ti + 1) * 128, c0:c0 + L], r=[k.dbuf["UT"]], w=[bh])
                    sc.dma("sp", gt[:, 0:L], k.UT.ap()[U_CG + ti * 128:U_CG + (ti + 1) * 128, c0:c0 + L], r=[k.dbuf["UT"]], w=[bg])
                    sc.op("pool", lambda e, L=L: e.memset(hp[:, 0:15], 0.0), w=[bh])
                    sc.op("pool", lambda e, L=L: e.memset(hp[:, 15 + L:30 + L], 0.0), w=[bh])
                    sc.op("act", lambda e, L=L: e.activation(out=gt[:, 0:L], in_=gt[:, 0:L], func=AF.Sigmoid), r=[bg], w=[bg])
                    sc.op("dve", lambda e, L=L: e.tensor_tensor(out=hp[:, 15:15 + L], in0=hp[:, 15:15 + L], in1=gt[:, 0:L], op=ALU.mult),
                          r=[bh, bg], w=[bh])
                    a = acc[:, ti, c0:c0 + L]
                    sc.op("dve", lambda e, a=a, L=L, ti=ti: e.tensor_scalar(out=a, in0=hp[:, 0:L], scalar1=cvp[:, ti, 0:1], scalar2=cvp[:, ti, 31:32],
                                                                            op0=ALU.mult, op1=ALU.add), r=[bh, bcv], w=[bacc])
                    for tp in range(1, 31):
                        sc.op("dve", lambda e, a=a, L=L, ti=ti, tp=tp: e.scalar_tensor_tensor(
                            out=a, in0=hp[:, tp:tp + L], scalar=cvp[:, ti, tp:tp + 1], in1=a, op0=ALU.mult, op1=ALU.add),
                            r=[bh, bcv, bacc], w=[bacc])
            if "yc" in k.debug:
                for ti in range(2):
                    sc.dma("sp", k.YT.ap()[512 + ti * 128:512 + (ti + 1) * 128, :], acc[:, ti, :], r=[bacc], w=[k.dbuf["YT"]])
            sc.barrier()
            sc.flush()
        ones = k.sb(st, [128, 128])
        bo = Buf()
        sc.op("pool", lambda e: e.memset(ones[:], 1.0), w=[bo])
        stat = [k.sb(st, [1, 2, T]) for _ in range(2)]
        bstat = [Buf(), Buf()]
        sq = [k.sb(st, [128, T]) for _ in range(2)]
        bsq = [Buf(), Buf()]
        p_s, p_q = k.ps(st), k.ps(st)
        bps, bpq = Buf(), Buf()
        for it, c0 in enumerate(range(0, LT, T)):
            tn = min(T, LT - c0)
            si = it % 2
            for ti in range(2):
                sc.op("pe", lambda e, ti=ti, c0=c0, tn=tn: e.matmul(p_s[:, 0:tn], lhsT=ones[:], rhs=acc[:, ti, c0:c0 + tn],
                                                                    start=(ti == 0), stop=(ti == 1)), r=[bo, bacc], w=[bps], inc=(ti == 1))
            for ti in range(2):
                sc.op("act", lambda e, ti=ti, c0=c0, tn=tn: e.activation(out=sq[ti][:, 0:tn], in_=acc[:, ti, c0:c0 + tn], func=AF.Square),
                      r=[bacc], w=[bsq[ti]])
                sc.op("pe", lambda e, ti=ti, tn=tn: e.matmul(p_q[:, 0:tn], lhsT=ones[:], rhs=sq[ti][:, 0:tn],
                                                             start=(ti == 0), stop=(ti == 1)), r=[bo, bsq[ti]], w=[bpq])
            sc.op("act", lambda e, si=si, tn=tn: e.activation(out=stat[si][0:1, 0, 0:tn], in_=p_s[0:1, 0:tn], func=AF.Copy), r=[bps], w=[bstat[si]])
            sc.op("act", lambda e, si=si, tn=tn: e.activation(out=stat[si][0:1, 1, 0:tn], in_=p_q[0:1, 0:tn], func=AF.Copy), r=[bpq], w=[bstat[si]])
            for a_ in range(2):
                sc.dma("sp", k.CST_loc.ap()[a_:a_ + 1, c0:c0 + tn], stat[si][0:1, a_, 0:tn], r=[bstat[si]], w=[k.dbuf["CST_loc"]])
        sc.cc("AllGather", ALU.bypass, GROUPS4, k.CST_loc.ap().opt(), k.CST.ap().opt(), r=[k.dbuf["CST_loc"]], w=[k.dbuf["CST_all"]])
        st8 = [k.sb(st, [128, 8, T]) for _ in range(2)]
        bst8 = [Buf(), Buf()]
        mean = k.sb(st, [128, T])
        rstd = k.sb(st, [128, T])
        bmr = Buf()
        zt = [k.sb(st, [128, T]) for _ in range(2)]
        zb = [k.sb(st, [128, T], BF16) for _ in range(2)]
        bzt = [Buf(), Buf()]
        bzb = [Buf(), Buf()]
        for it, c0 in enumerate(range(0, LT, T)):
            tn = min(T, LT - c0)
            i = it % 2
            src = bass.AP(k.CST, c0, [[0, 128], [LT, 8], [1, tn]])
            sc.dma("sp", st8[i][:, :, 0:tn], src, r=[k.dbuf["CST_all"]], w=[bst8[i]])
            v = st8[i]
            sc.op("dve", lambda e, v=v, tn=tn: e.tensor_tensor(out=mean[:, 0:tn], in0=v[:, 0, 0:tn], in1=v[:, 2, 0:tn], op=ALU.add), r=[bst8[i]], w=[bmr])
            sc.op("dve", lambda e, v=v, tn=tn: e.tensor_tensor(out=mean[:, 0:tn], in0=mean[:, 0:tn], in1=v[:, 4, 0:tn], op=ALU.add), r=[bst8[i], bmr], w=[bmr])
            sc.op("dve", lambda e, v=v, tn=tn: e.tensor_tensor(out=mean[:, 0:tn], in0=mean[:, 0:tn], in1=v[:, 6, 0:tn], op=ALU.add), r=[bst8[i], bmr], w=[bmr])
            sc.op("dve", lambda e, v=v, tn=tn: e.tensor_tensor(out=rstd[:, 0:tn], in0=v[:, 1, 0:tn], in1=v[:, 3, 0:tn], op=ALU.add), r=[bst8[i]], w=[bmr])
            sc.op("dve", lambda e, v=v, tn=tn: e.tensor_tensor(out=rstd[:, 0:tn], in0=rstd[:, 0:tn], in1=v[:, 5, 0:tn], op=ALU.add), r=[bst8[i], bmr], w=[bmr])
            sc.op("dve", lambda e, v=v, tn=tn: e.tensor_tensor(out=rstd[:, 0:tn], in0=rstd[:, 0:tn], in1=v[:, 7, 0:tn], op=ALU.add), r=[bst8[i], bmr], w=[bmr])
            sc.op("dve", lambda e, tn=tn: e.tensor_scalar(out=mean[:, 0:tn], in0=mean[:, 0:tn], scalar1=1.0 / 1024, scalar2=None, op0=ALU.mult), r=[bmr], w=[bmr])
            sc.op("dve", lambda e, v=v, tn=tn: e.tensor_tensor(out=v[:, 0, 0:tn], in0=mean[:, 0:tn], in1=mean[:, 0:tn], op=ALU.mult), r=[bmr], w=[bst8[i]])
            sc.op("dve", lambda e, v=v, tn=tn: e.scalar_tensor_tensor(out=rstd[:, 0:tn], in0=rstd[:, 0:tn], scalar=1.0 / 1024, in1=v[:, 0, 0:tn],
                                                                      op0=ALU.mult, op1=ALU.subtract), r=[bmr, bst8[i]], w=[bmr])
            sc.op("act", lambda e, tn=tn: e.activation(out=rstd[:, 0:tn], in_=rstd[:, 0:tn], func=AF.Sqrt, bias=EPS), r=[bmr], w=[bmr])
            sc.op("dve", lambda e, tn=tn: e.reciprocal(out=rstd[:, 0:tn], in_=rstd[:, 0:tn]), r=[bmr], w=[bmr])
            for ti in range(2):
                sc.op("pool", lambda e, ti=ti, c0=c0, tn=tn: e.tensor_tensor(out=zt[ti][:, 0:tn], in0=acc[:, ti, c0:c0 + tn], in1=mean[:, 0:tn], op=ALU.subtract),
                      r=[bacc, bmr], w=[bzt[ti]])
                sc.op("dve", lambda e, ti=ti, tn=tn: e.tensor_tensor(out=zt[ti][:, 0:tn], in0=zt[ti][:, 0:tn], in1=rstd[:, 0:tn], op=ALU.mult),
                      r=[bmr, bzt[ti]], w=[bzt[ti]])
                sc.op("act", lambda e, ti=ti, tn=tn: e.activation(out=zt[ti][:, 0:tn], in_=zt[ti][:, 0:tn], func=AF.Identity,
                                                                  scale=cvp[:, ti, 32:33], bias=cvp[:, ti, 33:34]), r=[bzt[ti], bcv], w=[bzt[ti]])
                sc.op("act", lambda e, ti=ti, tn=tn: e.activation(out=zb[ti][:, 0:tn], in_=zt[ti][:, 0:tn], func=AF.Silu), r=[bzt[ti]], w=[bzb[ti]])
                sc.dma("sp", k.ZT.ap()[512 + ti * 128:512 + (ti + 1) * 128, c0:c0 + tn], zb[ti][:, 0:tn], r=[bzb[ti]], w=[k.dbuf["ZT"]])
        sc.barrier()
        sc.flush()


def phase_fnet(k, l):
    cfg, sc = k.cfg, k.sc
    LT = cfg.LT
    with ExitStack() as st:
        pcol_i = k.sb(st, [128, 1], I32)
        bidx = Buf()
        sc.op("pool", lambda e: e.iota(pcol_i[:], pattern=[[0, 1]], base=0, channel_multiplier=1), w=[bidx])
        KGmax = min(2048, cfg.S)
        kf = k.sb(st, [128, KGmax])
        ki = k.sb(st, [128, KGmax], I32)
        sc.op("pool", lambda e: e.iota(ki[:], pattern=[[1, KGmax]], base=0, channel_multiplier=0), w=[bidx])
        sc.op("dve", lambda e: e.tensor_copy(out=kf[:], in_=ki[:]), r=[bidx], w=[bidx])
        CSD = k.sb(st, [128, 2, 512], BF16)
        bcsd = Buf()
        ncol = k.sb(st, [128, 2])
        nci = k.sb(st, [128, 2], I32)
        sc.op("pool", lambda e: e.iota(nci[:], pattern=[[128, 2]], base=0, channel_multiplier=1), w=[bidx])
        sc.op("dve", lambda e: e.tensor_copy(out=ncol[:], in_=nci[:]), r=[bidx], w=[bidx])
        tmpi = [k.sb(st, [128, KGmax], I32) for _ in range(4)]
        btmp = [Buf() for _ in range(4)]
        tcnt = [0]

        def gen(dst, ncolap, offap, M, n, bdst, extra_r=()):
            i = tcnt[0] % 4
            tcnt[0] += 1
            t = tmpi[i]
            sc.op("pool", lambda e: e.tensor_scalar(out=t[:, 0:n], in0=kf[:, 0:n], scalar1=ncolap, scalar2=offap,
                                                    op0=ALU.mult, op1=ALU.add), r=[bidx] + list(extra_r), w=[btmp[i]])
            sc.op("dve", lambda e: e.tensor_single_scalar(out=t[:, 0:n], in_=t[:, 0:n], scalar=M - 1, op=ALU.bitwise_and),
                  r=[btmp[i]], w=[btmp[i]])
            sc.op("act", lambda e: e.activation(out=dst, in_=t[:, 0:n], func=AF.Sin, scale=2 * PI / M, bias=-PI),
                  r=[btmp[i]], w=[bdst])

        for cc in range(2):
            gen(CSD[:, cc, 0:256], ncol[:, cc:cc + 1], 192.0, 256, 256, bcsd)
            gen(CSD[:, cc, 256:512], ncol[:, cc:cc + 1], 0.0, 256, 256, bcsd)
        pb = [k.ps(st) for _ in range(8)]
        pbb = [Buf() for _ in range(8)]
        cl = CastLoader(k, st, min(2048, cfg.S))

        def do_seq(c0, L):
            nch = L // 128
            KG = min(2048, L)
            KT = min(512, L)
            nkt = KG // KT
            ngr = L // KG
            scale = 1.0 / math.sqrt(256.0 * L)
            XT = k.sb(st, [128, 2, L], BF16)
            bxt = Buf()
            for cc in range(2):
                for x0 in range(0, L, 2048):
                    xn = min(2048, L - x0)
                    cl.load(XT[:, cc, x0:x0 + xn], k.UT.ap()[U_FN + cc * 128:U_FN + (cc + 1) * 128, c0 + x0:c0 + x0 + xn], 128, xn, k.dbuf["UT"], bxt)
            AB = k.sb(st, [128, nch, 512], BF16)
            bab = Buf()
            for t in range(nch):
                pi = t % 8
                for cc in range(2):
                    sc.op("pe", lambda e, pi=pi, cc=cc, t=t: e.matmul(pb[pi][:, :], lhsT=XT[:, cc, t * 128:(t + 1) * 128], rhs=CSD[:, cc, :],
                                                                      start=(cc == 0), stop=(cc == 1)), r=[bxt, bcsd], w=[pbb[pi]], inc=(cc == 1))
                eng = ("act", "dve")[t % 2]
                if eng == "act":
                    sc.op("act", lambda e, pi=pi, t=t: e.activation(out=AB[:, t, :], in_=pb[pi][:, :], func=AF.Copy), r=[pbb[pi]], w=[bab])
                else:
                    sc.op("dve", lambda e, pi=pi, t=t: e.tensor_copy(out=AB[:, t, :], in_=pb[pi][:, :]), r=[pbb[pi]], w=[bab])
            nrow = k.sb(st, [128, nch])
            nri = k.sb(st, [128, nch], I32)
            boff = Buf()
            sc.op("pool", lambda e, nri=nri, nch=nch: e.iota(nri[:], pattern=[[128, nch]], base=0, channel_multiplier=1), w=[boff])
            sc.op("dve", lambda e, nrow=nrow, nri=nri: e.tensor_copy(out=nrow[:], in_=nri[:]), r=[boff], w=[boff])
            offc = k.sb(st, [128, ngr, nch])
            offs = k.sb(st, [128, ngr, nch])
            oi = k.sb(st, [128, nch], I32)
            for g in range(ngr):
                sc.op("dve", lambda e, g=g, oi=oi, nrow=nrow: e.tensor_scalar(out=oi[:], in0=nrow[:], scalar1=float(g), scalar2=None, op0=ALU.mult),
                      r=[boff], w=[boff])
                sc.op("dve", lambda e, oi=oi, ngr=ngr: e.tensor_single_scalar(out=oi[:], in_=oi[:], scalar=ngr - 1, op=ALU.bitwise_and), r=[boff], w=[boff])
                sc.op("dve", lambda e, g=g, oi=oi, offc=offc, KG=KG, L=L: e.tensor_scalar(out=offc[:, g, :], in0=oi[:], scalar1=float(KG), scalar2=0.75 * L,
                                                                                       op0=ALU.mult, op1=ALU.add), r=[boff], w=[boff])
                sc.op("dve", lambda e, g=g, oi=oi, offs=offs, KG=KG, L=L: e.tensor_scalar(out=offs[:, g, :], in0=oi[:], scalar1=float(KG), scalar2=0.5 * L,
                                                                                       op0=ALU.mult, op1=ALU.add), r=[boff], w=[boff])
            E = [k.sb(st, [128, 2, KG], BF16) for _ in range(2)]
            bE = [Buf(), Buf()]
            yo = [k.sb(st, [128, KT], BF16) for _ in range(2)]
            byo = [Buf(), Buf()]
            ec = 0
            oc = 0
            for g in range(ngr):
                for c in range(nch):
                    ei = ec % 2
                    ec += 1
                    gen(E[ei][:, 0, :], nrow[:, c:c + 1], offc[:, g, c:c + 1], L, KG, bE[ei], extra_r=[boff])
                    gen(E[ei][:, 1, :], nrow[:, c:c + 1], offs[:, g, c:c + 1], L, KG, bE[ei], extra_r=[boff])
                    for h in range(2):
                        for kt in range(nkt):
                            pi = h * nkt + kt
                            sc.op("pe", lambda e, pi=pi, c=c, h=h, kt=kt, ei=ei: e.matmul(
                                pb[pi][:, 0:KT], lhsT=AB[:, c, h * 128:(h + 1) * 128], rhs=E[ei][:, 0, kt * KT:(kt + 1) * KT],
                                start=(c == 0), stop=False), r=[bab, bE[ei]], w=[pbb[pi]], inc=False)
                            sc.op("pe", lambda e, pi=pi, c=c, h=h, kt=kt, ei=ei: e.matmul(
                                pb[pi][:, 0:KT], lhsT=AB[:, c, 256 + h * 128:256 + (h + 1) * 128], rhs=E[ei][:, 1, kt * KT:(kt + 1) * KT],
                                start=False, stop=(c == nch - 1)), r=[bab, bE[ei]], w=[pbb[pi]], inc=True)
                for h in range(2):
                    for kt in range(nkt):
                        pi = h * nkt + kt
                        oi_ = oc % 2
                        oc += 1
                        sc.op("act", lambda e, pi=pi, oi_=oi_: e.activation(out=yo[oi_][:, :], in_=pb[pi][:, 0:KT], func=AF.Copy, scale=scale),
                              r=[pbb[pi]], w=[byo[oi_]])
                        col = c0 + g * KG + kt * KT
                        sc.dma("sp", k.ZT.ap()[h * 128:(h + 1) * 128, col:col + KT], yo[oi_][:, :], r=[byo[oi_]], w=[k.dbuf["ZT"]])

        for (c0_, L_) in k.seqs:
            do_seq(c0_, L_)
        sc.barrier()
        sc.flush()


def phase_wout(k, l):
    cfg, sc = k.cfg, k.sc
    D, KD, TL = cfg.D, cfg.KD, cfg.TL
    T = 512
    with ExitStack() as st:
        W = k.sb(st, [128, 8, D], BF16)
        bW = Buf()
        cl = CastLoader(k, st, min(2048, D))
        for cc in range(8):
            for d0 in range(0, D, 2048):
                dn = min(2048, D - d0)
                cl.load(W[:, cc, d0:d0 + dn], k.w_out.ap()[l, cc * 128:(cc + 1) * 128, d0:d0 + dn], 128, dn, k.dbuf["w_out_q"], bW)
        Z = [k.sb(st, [128, 8, T], BF16) for _ in range(2)]
        bZ = [Buf(), Buf()]
        o = [k.sb(st, [128, T]) for _ in range(2)]
        bo = [Buf(), Buf()]
        pb = [k.ps(st) for _ in range(4)]
        pbb = [Buf() for _ in range(4)]
        zv = k.ZT.ap().rearrange("(c p) t -> p c t", p=128)
        ti = 0
        oc = 0
        for rb in range(4):
            for (c0, tn, lat) in tok_tiles(cfg, T):
                zi = ti % 2
                ti += 1
                col = seq_col(cfg, rb, c0, lat)
                sc.dma("sp", Z[zi][:, :, 0:tn], zv[:, :, col:col + tn], r=[k.dbuf["ZT"]], w=[bZ[zi]])
                for dc in range(KD):
                    pi = oc % 4
                    oi = oc % 2
                    oc += 1
                    for cc in range(8):
                        sc.op("pe", lambda e, pi=pi, cc=cc, dc=dc, zi=zi, tn=tn: e.matmul(
                            pb[pi][:, 0:tn], lhsT=W[:, cc, dc * 128:(dc + 1) * 128], rhs=Z[zi][:, cc, 0:tn],
                            start=(cc == 0), stop=(cc == 7)), r=[bW, bZ[zi]], w=[pbb[pi]], inc=(cc == 7))
                    if oc % 2:
                        sc.op("act", lambda e, pi=pi, oi=oi, tn=tn: e.activation(out=o[oi][:, 0:tn], in_=pb[pi][:, 0:tn], func=AF.Copy),
                              r=[pbb[pi]], w=[bo[oi]])
                    else:
                        sc.op("dve", lambda e, pi=pi, oi=oi, tn=tn: e.tensor_copy(out=o[oi][:, 0:tn], in_=pb[pi][:, 0:tn]),
                              r=[pbb[pi]], w=[bo[oi]])
                    sc.dma("sp", k.MIXP.ap()[rb * D + dc * 128:rb * D + (dc + 1) * 128, c0:c0 + tn], o[oi][:, 0:tn],
                           r=[bo[oi]], w=[k.dbuf["MIXP"]])
        sc.cc("ReduceScatter", ALU.add, GROUPS4, k.MIXP.ap().opt(), k.MIX.ap().opt(), r=[k.dbuf["MIXP"]], w=[k.dbuf["MIX_loc"]])
        sc.barrier()
        sc.flush()


def ssd_setup(k):
    cfg = k.cfg
    LT = cfg.LT
    k.SXS = k.dscr("SXS", [256, LT])
    k.SSZ = k.dscr("SSZ", [256, LT])
    k.SBT = k.dscr("SBT", [128, LT], BF16)
    k.SCT = k.dscr("SCT", [128, LT], BF16)
    k.SBF = k.dscr("SBF", [128, LT])
    k.YTOK = k.dscr("YTOK", [LT, 256])
    k.ssdcw = k.din("ssdcw", [cfg.depth, 128, 24])
    k.ssdrow = k.din("ssdrow", [cfg.depth, 128, 528])


def phase_ssd_prep(k, l):
    cfg, sc = k.cfg, k.sc
    with ExitStack() as st:
        cw = k.sb(st, [128, 4, 6])
        bcw = Buf()
        sc.dma("sp", cw[:].rearrange("p a b -> p (a b)"), k.ssdcw.ap()[l], r=[k.dbuf["ssdcw"]], w=[bcw])
        Lm = cfg.S
        hp = [k.sb(st, [128, Lm + 4]) for _ in range(2)]
        bh = [Buf(), Buf()]
        acc = [k.sb(st, [128, Lm]) for _ in range(2)]
        ba = [Buf(), Buf()]
        ob = [k.sb(st, [128, Lm], BF16) for _ in range(2)]
        bob = [Buf(), Buf()]
        it = 0
        srcrow = [U_X, U_X + 128, U_B, U_C]
        for t in range(4):
            for (c0, L) in k.seqs:
                i = it % 2
                it += 1
                sc.dma("sp", hp[i][:, 2:2 + L], k.UT.ap()[srcrow[t]:srcrow[t] + 128, c0:c0 + L], r=[k.dbuf["UT"]], w=[bh[i]])
                sc.op("pool", lambda e, i=i: e.memset(hp[i][:, 0:2], 0.0), w=[bh[i]])
                sc.op("pool", lambda e, i=i, L=L: e.memset(hp[i][:, 2 + L:4 + L], 0.0), w=[bh[i]])
                a = acc[i]
                sc.op("dve", lambda e, a=a, i=i, L=L, t=t: e.tensor_scalar(out=a[:, 0:L], in0=hp[i][:, 0:L], scalar1=cw[:, t, 0:1], scalar2=cw[:, t, 5:6],
                                                                          op0=ALU.mult, op1=ALU.add), r=[bh[i], bcw], w=[ba[i]])
                for tp in range(1, 5):
                    sc.op("dve", lambda e, a=a, i=i, L=L, t=t, tp=tp: e.scalar_tensor_tensor(
                        out=a[:, 0:L], in0=hp[i][:, tp:tp + L], scalar=cw[:, t, tp:tp + 1], in1=a[:, 0:L], op0=ALU.mult, op1=ALU.add),
                        r=[bh[i], bcw, ba[i]], w=[ba[i]])
                if t < 2:
                    sc.op("act", lambda e, a=a, L=L: e.activation(out=a[:, 0:L], in_=a[:, 0:L], func=AF.Silu), r=[ba[i]], w=[ba[i]])
                    sc.dma("sp", k.SXS.ap()[t * 128:(t + 1) * 128, c0:c0 + L], a[:, 0:L], r=[ba[i]], w=[k.dbuf["SXS"]])
                else:
                    sc.op("act", lambda e, a=a, i=i, L=L: e.activation(out=ob[i][:, 0:L], in_=a[:, 0:L], func=AF.Silu), r=[ba[i]], w=[bob[i]])
                    dst = k.SBT if t == 2 else k.SCT
                    sc.dma("sp", dst.ap()[:, c0:c0 + L], ob[i][:, 0:L], r=[bob[i]], w=[k.dbuf[dst.name]])
                    if t == 2:
                        sc.op("act", lambda e, a=a, L=L: e.activation(out=a[:, 0:L], in_=a[:, 0:L], func=AF.Silu), r=[ba[i], bob[i]], w=[ba[i]])
                        sc.dma("sp", k.SBF.ap()[:, c0:c0 + L], a[:, 0:L], r=[ba[i]], w=[k.dbuf["SBF"]])
        for t in range(2):
            for (c0, L) in k.seqs:
                i = it % 2
                it += 1
                a = acc[i]
                sc.dma("sp", a[:, 0:L], k.UT.ap()[U_Z + t * 128:U_Z + (t + 1) * 128, c0:c0 + L], r=[k.dbuf["UT"]], w=[ba[i]])
                sc.op("act", lambda e, a=a, L=L: e.activation(out=a[:, 0:L], in_=a[:, 0:L], func=AF.Silu), r=[ba[i]], w=[ba[i]])
                sc.dma("sp", k.SSZ.ap()[t * 128:(t + 1) * 128, c0:c0 + L], a[:, 0:L], r=[ba[i]], w=[k.dbuf["SSZ"]])
        sc.barrier()
        sc.flush()


def phase_ssd_scan(k, l, d):
    cfg, sc = k.cfg, k.sc
    S, CT = cfg.S, cfg.CT
    nl, ncx = S // 128, CT // 128
    if d == 0:
        order = [(S + c * 128) for c in range(ncx)] + [c * 128 for c in range(nl)]
    else:
        order = [(S + c * 128) for c in reversed(range(ncx))] + [c * 128 for c in reversed(range(nl))]
    NEG = -30000.0
    with ExitStack() as st:
        ii = k.sb(st, [128, 128], I32)
        jj = k.sb(st, [128, 128], I32)
        bc0 = Buf()
        sc.op("pool", lambda e: e.iota(ii[:], pattern=[[0, 128]], base=0, channel_multiplier=1), w=[bc0])
        sc.op("pool", lambda e: e.iota(jj[:], pattern=[[1, 128]], base=0, channel_multiplier=0), w=[bc0])
        tri = k.sb(st, [128, 128])
        mneg = k.sb(st, [128, 128])
        ones = k.sb(st, [128, 128])
        sc.op("dve", lambda e: e.tensor_tensor(out=tri[:], in0=jj[:], in1=ii[:], op=(ALU.is_ge if d == 0 else ALU.is_le)), r=[bc0], w=[bc0])
        sc.op("dve", lambda e: e.tensor_scalar(out=mneg[:], in0=tri[:], scalar1=-1.0, scalar2=-NEG, op0=ALU.add, op1=ALU.mult), r=[bc0], w=[bc0])
        sc.op("pool", lambda e: e.memset(ones[:], 1.0), w=[bc0])
        sel = k.sb(st, [4, 4, 128])
        si1 = k.sb(st, [4, 4, 128], I32)
        si2 = k.sb(st, [4, 4, 128], I32)
        sc.op("pool", lambda e: e.iota(si1[:], pattern=[[0, 4], [0, 128]], base=0, channel_multiplier=1), w=[bc0])
        sc.op("pool", lambda e: e.iota(si2[:], pattern=[[1, 4], [0, 128]], base=0, channel_multiplier=0), w=[bc0])
        sc.op("dve", lambda e: e.tensor_tensor(out=sel[:], in0=si1[:], in1=si2[:], op=ALU.is_equal), r=[bc0], w=[bc0])
        row = k.sb(st, [128, 528])
        sc.dma("sp", row[:], k.ssdrow.ap()[l], r=[k.dbuf["ssdrow"]], w=[bc0])
        Arow = k.sb(st, [128, 4])
        sc.op("act", lambda e: e.activation(out=Arow[:], in_=row[:, 8 + 4 * d:12 + 4 * d], func=AF.Exp), r=[bc0], w=[bc0])
        sc.op("dve", lambda e: e.tensor_scalar(out=Arow[:], in0=Arow[:], scalar1=-1.0, scalar2=None, op0=ALU.mult), r=[bc0], w=[bc0])
        ST = k.sb(st, [128, 256])
        bST = Buf()
        sc.op("pool", lambda e: e.memset(ST[:], 0.0), w=[bST])
        NB = 2
        mk = lambda shape, dt=F32: [k.sb(st, shape, dt) for _ in range(NB)]
        xsT, bxsT = mk([128, 2, 128]), [Buf() for _ in range(NB)]
        bT, bbT = mk([128, 128], BF16), [Buf() for _ in range(NB)]
        bF, bbF = mk([128, 128]), [Buf() for _ in range(NB)]
        cT, bcT = mk([128, 128], BF16), [Buf() for _ in range(NB)]
        dtr, bdtr = mk([4, 128]), [Buf() for _ in range(NB)]
        szT, bszT = mk([128, 2, 128]), [Buf() for _ in range(NB)]
        xtok, bxtok = mk([128, 256]), [Buf() for _ in range(NB)]
        btok, bbtok = mk([128, 128], BF16), [Buf() for _ in range(NB)]
        dtv, bdtv = mk([128, 4]), [Buf() for _ in range(NB)]
        dtA, bdtA = mk([128, 4]), [Buf() for _ in range(NB)]
        cum, bcum = mk([128, 4]), [Buf() for _ in range(NB)]
        tot, btot = mk([128, 4]), [Buf() for _ in range(NB)]
        de, bde = mk([128, 4]), [Buf() for _ in range(NB)]
        ein, bein = mk([128, 4]), [Buf() for _ in range(NB)]
        cumT, bcumT = mk([4, 128]), [Buf() for _ in range(NB)]
        seg, bseg = mk([128, 4, 128]), [Buf() for _ in range(NB)]
        Lm, bLm = mk([128, 4, 128], BF16), [Buf() for _ in range(NB)]
        scT, bscT = mk([128, 128], BF16), [Buf() for _ in range(NB)]
        WT, bWT = mk([128, 4, 128], BF16), [Buf() for _ in range(NB)]
        xdt, bxdt = mk([128, 256], BF16), [Buf() for _ in range(NB)]
        xdd, bxdd = mk([128, 256], BF16), [Buf() for _ in range(NB)]
        Sb, bSb = mk([128, 256], BF16), [Buf() for _ in range(NB)]
        yt, byt = mk([128, 256]), [Buf() for _ in range(NB)]
        yprev, byprev = mk([128, 256]), [Buf() for _ in range(NB)]
        szt, bszt = mk([128, 256]), [Buf() for _ in range(NB)]
        ms, bms = mk([128, 1]), [Buf() for _ in range(NB)]
        junk, bjunk = mk([128, 256]), [Buf() for _ in range(NB)]
        zo, bzo = mk([128, 2, 128], BF16), [Buf() for _ in range(NB)]
        pT = k.ps(st)
        pS = k.ps(st)
        pBC = k.ps(st)
        pSC = k.ps(st)
        pYD = k.ps(st)
        pZ = k.ps(st)
        bpT, bpS, bpBC, bpSC, bpYD, bpZ = Buf(), Buf(), Buf(), Buf(), Buf(), Buf()
        psum_bufs = (bpT, bpS, bpBC, bpSC, bpYD, bpZ)
        dcol = U_DTF if d == 0 else U_DTB

        class _S:
            dma = sc.dma
            barrier = sc.barrier
            flush = sc.flush

            @staticmethod
            def op(eng, fn, r=(), w=(), inc=True):
                if eng != "pe":
                    w = list(w) + [b for b in r if b in psum_bufs]
                k.sc.op(eng, fn, r=r, w=w, inc=inc)
        sc = _S
        for ci, col in enumerate(order):
            i = ci % NB
            sc.dma("sp", xsT[i][:], k.SXS.ap()[:, col:col + 128].rearrange("(c p) t -> p c t", p=128), r=[k.dbuf["SXS"]], w=[bxsT[i]])
            sc.dma("sp", bT[i][:], k.SBT.ap()[:, col:col + 128], r=[k.dbuf["SBT"]], w=[bbT[i]])
            sc.dma("sp", bF[i][:], k.SBF.ap()[:, col:col + 128], r=[k.dbuf["SBF"]], w=[bbF[i]])
            sc.dma("sp", cT[i][:], k.SCT.ap()[:, col:col + 128], r=[k.dbuf["SCT"]], w=[bcT[i]])
            sc.dma("sp", dtr[i][:], k.UT.ap()[dcol:dcol + 4, col:col + 128], r=[k.dbuf["UT"]], w=[bdtr[i]])
            if SSD_LIM == 1:
                continue
            SK = os.environ.get("SSD_SKIP", "")
            for t in range(2):
                if "x" in SK:
                    break
                sc.op("pe", lambda e, i=i, t=t: e.transpose(out=pT[:, t * 128:(t + 1) * 128], in_=xsT[i][:, t, :], identity=k.ident[:]),
                      r=[bxsT[i], k.ident_b], w=[bpT])
            if "b" not in SK:
                sc.op("pe", lambda e, i=i: e.transpose(out=pT[:, 256:384], in_=bF[i][:], identity=k.ident[:]), r=[bbF[i], k.ident_b], w=[bpT])
            if "d" not in SK:
                sc.op("pe", lambda e, i=i: e.matmul(pT[:, 384:388], lhsT=dtr[i][:], rhs=k.ident[0:4, 0:4], start=True, stop=True), r=[bdtr[i], k.ident_b], w=[bpT])
            if "e" not in SK:
                sc.op("act", lambda e, i=i: e.activation(out=xtok[i][:], in_=pT[:, 0:256], func=AF.Copy), r=[bpT], w=[bxtok[i]])
            if "f" not in SK:
                sc.op("dve", lambda e, i=i: e.tensor_copy(out=btok[i][:], in_=pT[:, 256:384]), r=[bpT], w=[bbtok[i]])
            if SSD_LIM == 2:
                continue
            sc.op("dve", lambda e, i=i: e.tensor_tensor(out=dtv[i][:], in0=pT[:, 384:388], in1=row[:, 4 * d:4 * d + 4], op=ALU.add), r=[bpT, bc0], w=[bdtv[i]])
            sc.op("act", lambda e, i=i: e.activation(out=dtv[i][:], in_=dtv[i][:], func=AF.Exp), r=[bdtv[i]], w=[bdtv[i]])
            sc.op("act", lambda e, i=i: e.activation(out=dtv[i][:], in_=dtv[i][:], func=AF.Ln, bias=1.0), r=[bdtv[i]], w=[bdtv[i]])
            sc.op("dve", lambda e, i=i: e.tensor_tensor(out=dtA[i][:], in0=dtv[i][:], in1=Arow[:], op=ALU.mult), r=[bdtv[i], bc0], w=[bdtA[i]])
            if SSD_LIM == 3:
                continue
            sc.op("pe", lambda e, i=i: e.matmul(pS[:, 0:4], lhsT=tri[:], rhs=dtA[i][:], start=True, stop=True), r=[bc0, bdtA[i]], w=[bpS])
            sc.op("pe", lambda e, i=i: e.matmul(pS[:, 4:8], lhsT=ones[:], rhs=dtA[i][:], start=True, stop=True), r=[bc0, bdtA[i]], w=[bpS])
            sc.op("dve", lambda e, i=i: e.tensor_copy(out=cum[i][:], in_=pS[:, 0:4]), r=[bpS], w=[bcum[i]])
            sc.op("act", lambda e, i=i: e.activation(out=tot[i][:], in_=pS[:, 4:8], func=AF.Exp), r=[bpS], w=[btot[i]])
            sc.op("dve", lambda e, i=i: e.tensor_tensor(out=de[i][:], in0=pS[:, 4:8], in1=cum[i][:], op=ALU.subtract), r=[bpS, bcum[i]], w=[bde[i]])
            sc.op("act", lambda e, i=i: e.activation(out=de[i][:], in_=de[i][:], func=AF.Exp), r=[bde[i]], w=[bde[i]])
            sc.op("act", lambda e, i=i: e.activation(out=ein[i][:], in_=cum[i][:], func=AF.Exp), r=[bcum[i]], w=[bein[i]])
            if SSD_LIM == 4:
                continue
            sc.op("pe", lambda e, i=i: e.matmul(pS[0:4, 8:136], lhsT=cum[i][:], rhs=k.ident[:], start=True, stop=True), r=[bcum[i], k.ident_b], w=[bpS])
            sc.op("dve", lambda e, i=i: e.tensor_copy(out=cumT[i][:], in_=pS[0:4, 8:136]), r=[bpS], w=[bcumT[i]])
            for h in range(4):
                sc.op("pe", lambda e, i=i, h=h: e.matmul(pBC[:, h * 128:(h + 1) * 128], lhsT=sel[:, h, :], rhs=cumT[i][:], start=True, stop=True),
                      r=[bc0, bcumT[i]], w=[bpBC])
            if SSD_LIM == 5:
                continue
            for h in range(4):
                sc.op("dve", lambda e, i=i, h=h: e.scalar_tensor_tensor(out=seg[i][:, h, :], in0=pBC[:, h * 128:(h + 1) * 128], scalar=cum[i][:, h:h + 1],
                                                                        in1=mneg[:], op0=ALU.subtract, op1=ALU.add), r=[bpBC, bcum[i], bc0], w=[bseg[i]])
            sc.op("act", lambda e, i=i: e.activation(out=Lm[i][:], in_=seg[i][:], func=AF.Exp), r=[bseg[i]], w=[bLm[i]])
            if SSD_LIM == 6:
                continue
            sc.op("pe", lambda e, i=i: e.matmul(pSC[:, 0:128], lhsT=bT[i][:], rhs=cT[i][:], start=True, stop=True), r=[bbT[i], bcT[i]], w=[bpSC])
            sc.op("act", lambda e, i=i: e.activation(out=scT[i][:], in_=pSC[:, 0:128], func=AF.Copy), r=[bpSC], w=[bscT[i]])
            scb = sbap(scT[i], 0, 128, 0, [[0, 4], [1, 128]])
            sc.op("dve", lambda e, i=i, scb=scb: e.tensor_tensor(out=WT[i][:], in0=Lm[i][:], in1=scb, op=ALU.mult), r=[bLm[i], bscT[i]], w=[bWT[i]])
            if SSD_LIM == 7:
                continue
            dtb = sbap(dtv[i], 0, 128, 0, [[1, 4], [0, 64]])
            deb = sbap(de[i], 0, 128, 0, [[1, 4], [0, 64]])
            sc.op("pool", lambda e, i=i, dtb=dtb: e.tensor_tensor(out=xdt[i][:].rearrange("p (h c) -> p h c", c=64), in0=xtok[i][:].rearrange("p (h c) -> p h c", c=64),
                                                                  in1=dtb, op=ALU.mult), r=[bxtok[i], bdtv[i]], w=[bxdt[i]])
            sc.op("pool", lambda e, i=i, deb=deb: e.tensor_tensor(out=xdd[i][:].rearrange("p (h c) -> p h c", c=64), in0=xdt[i][:].rearrange("p (h c) -> p h c", c=64),
                                                                  in1=deb, op=ALU.mult), r=[bxdt[i], bde[i]], w=[bxdd[i]])
            for h in range(4):
                sc.op("pe", lambda e, i=i, h=h: e.matmul(pYD[:, h * 64:(h + 1) * 64], lhsT=WT[i][:, h, :], rhs=xdt[i][:, h * 64:(h + 1) * 64], start=True, stop=True),
                      r=[bWT[i], bxdt[i]], w=[bpYD], inc=(h == 3))
            if SSD_LIM == 8:
                continue
            sc.op("act", lambda e, i=i: e.activation(out=Sb[i][:], in_=ST[:], func=AF.Copy), r=[bST], w=[bSb[i]])
            sc.op("pe", lambda e, i=i: e.matmul(pYD[:, 256:512], lhsT=cT[i][:], rhs=Sb[i][:], start=True, stop=True), r=[bcT[i], bSb[i]], w=[bpYD])
            sc.op("pe", lambda e, i=i: e.matmul(pSC[:, 128:384], lhsT=btok[i][:], rhs=xdd[i][:], start=True, stop=True), r=[bbtok[i], bxdd[i]], w=[bpSC])
            for h in range(4):
                sc.op("dve", lambda e, i=i, h=h: e.scalar_tensor_tensor(out=ST[:, h * 64:(h + 1) * 64], in0=ST[:, h * 64:(h + 1) * 64], scalar=tot[i][:, h:h + 1],
                                                                        in1=pSC[:, 128 + h * 64:128 + (h + 1) * 64], op0=ALU.mult, op1=ALU.add),
                      r=[bST, btot[i], bpSC, bSb[i]], w=[bST])
            if SSD_LIM == 9:
                continue
            einb = sbap(ein[i], 0, 128, 0, [[1, 4], [0, 64]])
            sc.op("dve", lambda e, i=i, einb=einb: e.tensor_tensor(out=yt[i][:].rearrange("p (h c) -> p h c", c=64), in0=pYD[:, 256:512].rearrange("p (h c) -> p h c", c=64),
                                                                   in1=einb, op=ALU.mult), r=[bpYD, bein[i]], w=[byt[i]])
            sc.op("dve", lambda e, i=i: e.tensor_tensor(out=yt[i][:], in0=yt[i][:], in1=pYD[:, 0:256], op=ALU.add), r=[bpYD, byt[i]], w=[byt[i]])
            if SSD_LIM == 10:
                continue
            if d == 0:
                sc.op("pool", lambda e, i=i: e.tensor_tensor(out=junk[i][:], in0=xtok[i][:], in1=row[:, 16:272], op=ALU.mult), r=[bxtok[i], bc0], w=[bjunk[i]])
                sc.op("pool", lambda e, i=i: e.tensor_tensor(out=yt[i][:], in0=yt[i][:], in1=junk[i][:], op=ALU.add), r=[byt[i], bjunk[i]], w=[byt[i]])
                sc.dma("sp", k.YTOK.ap()[col:col + 128, :], yt[i][:], r=[byt[i]], w=[k.dbuf["YTOK"]])
            else:
                sc.dma("sp", yprev[i][:], k.YTOK.ap()[col:col + 128, :], r=[k.dbuf["YTOK"]], w=[byprev[i]])
                sc.dma("sp", szT[i][:], k.SSZ.ap()[:, col:col + 128].rearrange("(c p) t -> p c t", p=128), r=[k.dbuf["SSZ"]], w=[bszT[i]])
                for t in range(2):
                    sc.op("pe", lambda e, i=i, t=t: e.transpose(out=pZ[:, t * 128:(t + 1) * 128], in_=szT[i][:, t, :], identity=k.ident[:]),
                          r=[bszT[i], k.ident_b], w=[bpZ])
                sc.op("pool", lambda e, i=i: e.tensor_tensor(out=yt[i][:], in0=yt[i][:], in1=yprev[i][:], op=ALU.add), r=[byt[i], byprev[i]], w=[byt[i]])
                sc.op("dve", lambda e, i=i: e.tensor_tensor(out=yt[i][:], in0=yt[i][:], in1=pZ[:, 0:256], op=ALU.mult), r=[byt[i], bpZ], w=[byt[i]])
                sc.op("act", lambda e, i=i: e.activation(out=junk[i][:], in_=yt[i][:], func=AF.Square, accum_out=ms[i][:]), r=[byt[i]], w=[bjunk[i], bms[i]])
                sc.op("act", lambda e, i=i: e.activation(out=ms[i][:], in_=ms[i][:], func=AF.Sqrt, scale=1.0 / 256, bias=EPS), r=[bms[i]], w=[bms[i]])
                sc.op("dve", lambda e, i=i: e.reciprocal(out=ms[i][:], in_=ms[i][:]), r=[bms[i]], w=[bms[i]])
                sc.op("dve", lambda e, i=i: e.scalar_tensor_tensor(out=yt[i][:], in0=yt[i][:], scalar=ms[i][:, 0:1], in1=row[:, 272:528], op0=ALU.mult, op1=ALU.mult),
                      r=[byt[i], bms[i], bc0], w=[byt[i]])
                for t in range(2):
                    sc.op("pe", lambda e, i=i, t=t: e.transpose(out=pZ[:, 256 + t * 128:256 + (t + 1) * 128], in_=yt[i][:, t * 128:(t + 1) * 128], identity=k.ident[:]),
                          r=[byt[i], k.ident_b], w=[bpZ])
                sc.op("act", lambda e, i=i: e.activation(out=zo[i][:].rearrange("p c t -> p (c t)"), in_=pZ[:, 256:512], func=AF.Copy), r=[bpZ], w=[bzo[i]])
                sc.dma("sp", k.ZT.ap()[768:1024, col:col + 128].rearrange("(c p) t -> p c t", p=128), zo[i][:], r=[bzo[i]], w=[k.dbuf["ZT"]])
        sc.barrier()
        sc.flush()


def phase_ssd(k, l):
    phase_ssd_prep(k, l)
    phase_ssd_scan(k, l, 0)
    phase_ssd_scan(k, l, 1)


def s5_setup(k):
    cfg = k.cfg
    k.s5p = k.din("s5p", [cfg.depth, 128, 1120])
    k.s5d = k.din("s5d", [cfg.depth, 16, 16])
    k.s5glu = k.din("s5glu_q", [cfg.depth, 256, 1024])
    k.GLUP = k.dscr("GLUP", [1024, cfg.LT])
    k.GLUL = k.dscr("GLU_loc", [256, cfg.LT])


def phase_s5(k, l):
    cfg, sc = k.cfg, k.sc
    S, CT, LT = cfg.S, cfg.CT, cfg.LT
    nlb, ncb = S // 8, CT // 8
    NB = nlb + ncb
    nsteps = max(1, math.ceil(math.log2(NB)))
    with ExitStack() as st:
        PAR = k.sb(st, [128, 1120])
        bpar = Buf()
        sc.dma("sp", PAR[:], k.s5p.ap()[l], r=[k.dbuf["s5p"]], w=[bpar])
        Dg = k.sb(st, [16, 16])
        sc.dma("sp", Dg[:], k.s5d.ap()[l], r=[k.dbuf["s5d"]], w=[bpar])
        lamr, lami, logdt = PAR[:, 0:32], PAR[:, 32:64], PAR[:, 64:96]
        Br, Bi, Cr, Ci = PAR[:, 96:352], PAR[:, 352:608], PAR[:, 608:864], PAR[:, 864:1120]
        bq = Buf()
        W = 32

        def T(n=W, dt=F32):
            return k.sb(st, [128, n], dt)

        def dv(fn, r=(), eng="dve"):
            sc.op(eng, fn, r=[bpar, bq] + list(r), w=[bq])

        pidx = T(1, I32)
        mlo, mhi, sgn = T(1), T(1), T(1)
        dv(lambda e: e.iota(pidx[:], pattern=[[0, 1]], base=0, channel_multiplier=1), eng="pool")
        dv(lambda e: e.tensor_single_scalar(out=mhi[:], in_=pidx[:], scalar=64, op=ALU.is_ge))
        dv(lambda e: e.tensor_scalar(out=mlo[:], in0=mhi[:], scalar1=-1.0, scalar2=1.0, op0=ALU.mult, op1=ALU.add))
        dv(lambda e: e.tensor_tensor(out=sgn[:], in0=mlo[:], in1=mhi[:], op=ALU.subtract))
        dtv, zr, zi, mag = T(), T(), T(), T()
        dv(lambda e: e.activation(out=dtv[:], in_=logdt, func=AF.Exp), eng="act")
        dv(lambda e: e.tensor_tensor(out=zr[:], in0=lamr, in1=dtv[:], op=ALU.mult))
        dv(lambda e: e.tensor_tensor(out=zi[:], in0=lami, in1=dtv[:], op=ALU.mult))
        dv(lambda e: e.activation(out=mag[:], in_=zr[:], func=AF.Exp), eng="act")

        def trig(dst, src, is_cos):
            s_, si, sf, ng = T(), T(W, I32), T(), T()
            dv(lambda e: e.tensor_scalar(out=s_[:], in0=src, scalar1=1.0 / (2 * PI), scalar2=(0.75 if is_cos else 0.5), op0=ALU.mult, op1=ALU.add))
            dv(lambda e: e.tensor_copy(out=si[:], in_=s_[:]))
            dv(lambda e: e.tensor_copy(out=sf[:], in_=si[:]))
            dv(lambda e: e.tensor_tensor(out=s_[:], in0=s_[:], in1=sf[:], op=ALU.subtract))
            dv(lambda e: e.tensor_single_scalar(out=ng[:], in_=s_[:], scalar=0.0, op=ALU.is_lt))
            dv(lambda e: e.tensor_tensor(out=s_[:], in0=s_[:], in1=ng[:], op=ALU.add))
            dv(lambda e: e.activation(out=dst, in_=s_[:], func=AF.Sin, scale=2 * PI, bias=-PI), eng="act")

        cz, sz = T(), T()
        trig(cz[:], zi[:], True)
        trig(sz[:], zi[:], False)
        PWr = k.sb(st, [128, 9, W])
        PWi = k.sb(st, [128, 9, W])
        dv(lambda e: e.memset(PWr[:, 0, :], 1.0), eng="pool")
        dv(lambda e: e.memset(PWi[:, 0, :], 0.0), eng="pool")
        dv(lambda e: e.tensor_tensor(out=PWr[:, 1, :], in0=mag[:], in1=cz[:], op=ALU.mult))
        dv(lambda e: e.tensor_tensor(out=PWi[:, 1, :], in0=mag[:], in1=sz[:], op=ALU.mult))
        t1, t2 = T(), T()

        def cmul(outr, outi, ar, ai, br, bi):
            dv(lambda e: e.tensor_tensor(out=t1[:], in0=ar, in1=br, op=ALU.mult))
            dv(lambda e: e.tensor_tensor(out=t2[:], in0=ai, in1=bi, op=ALU.mult))
            dv(lambda e: e.tensor_tensor(out=outr, in0=t1[:], in1=t2[:], op=ALU.subtract))
            dv(lambda e: e.tensor_tensor(out=t1[:], in0=ar, in1=bi, op=ALU.mult))
            dv(lambda e: e.tensor_tensor(out=t2[:], in0=ai, in1=br, op=ALU.mult))
            dv(lambda e: e.tensor_tensor(out=outi, in0=t1[:], in1=t2[:], op=ALU.add))

        for e_ in range(2, 9):
            cmul(PWr[:, e_, :], PWi[:, e_, :], PWr[:, e_ - 1, :], PWi[:, e_ - 1, :], PWr[:, 1, :], PWi[:, 1, :])
        SPr = k.sb(st, [128, nsteps, W])
        SPi = k.sb(st, [128, nsteps, W])
        SPj = k.sb(st, [128, nsteps, W])
        dv(lambda e: e.tensor_copy(out=SPr[:, 0, :], in_=PWr[:, 8, :]))
        dv(lambda e: e.tensor_copy(out=SPi[:, 0, :], in_=PWi[:, 8, :]))
        for s_ in range(1, nsteps):
            cmul(SPr[:, s_, :], SPi[:, s_, :], SPr[:, s_ - 1, :], SPi[:, s_ - 1, :], SPr[:, s_ - 1, :], SPi[:, s_ - 1, :])
        dv(lambda e: e.tensor_scalar(out=SPj[:].rearrange("p a b -> p (a b)"), in0=SPi[:].rearrange("p a b -> p (a b)"), scalar1=sgn[:, 0:1], scalar2=None, op0=ALU.mult))
        nr, den, cr, ci = T(), T(), T(), T()
        dv(lambda e: e.tensor_scalar(out=nr[:], in0=PWr[:, 1, :], scalar1=-1.0, scalar2=None, op0=ALU.add))
        dv(lambda e: e.tensor_tensor(out=den[:], in0=lamr, in1=lamr, op=ALU.mult))
        dv(lambda e: e.tensor_tensor(out=t1[:], in0=lami, in1=lami, op=ALU.mult))
        dv(lambda e: e.tensor_tensor(out=den[:], in0=den[:], in1=t1[:], op=ALU.add))
        dv(lambda e: e.reciprocal(out=den[:], in_=den[:]))
        dv(lambda e: e.tensor_tensor(out=t1[:], in0=nr[:], in1=lamr, op=ALU.mult))
        dv(lambda e: e.tensor_tensor(out=t2[:], in0=PWi[:, 1, :], in1=lami, op=ALU.mult))
        dv(lambda e: e.tensor_tensor(out=cr[:], in0=t1[:], in1=t2[:], op=ALU.add))
        dv(lambda e: e.tensor_tensor(out=cr[:], in0=cr[:], in1=den[:], op=ALU.mult))
        dv(lambda e: e.tensor_tensor(out=t1[:], in0=PWi[:, 1, :], in1=lamr, op=ALU.mult))
        dv(lambda e: e.tensor_tensor(out=t2[:], in0=nr[:], in1=lami, op=ALU.mult))
        dv(lambda e: e.tensor_tensor(out=ci[:], in0=t1[:], in1=t2[:], op=ALU.subtract))
        dv(lambda e: e.tensor_tensor(out=ci[:], in0=ci[:], in1=den[:], op=ALU.mult))
        X1 = k.sb(st, [128, 2, 256])
        X2 = k.sb(st, [128, 2, 256])
        bpr, bpi, u1, u2 = T(256), T(256), T(256), T(256)

        def bc16(t, d):
            return sbap(t, 0, 128, d * 16, [[1, 16], [0, 16]])

        v3 = lambda ap: ap.rearrange("p (g c) -> p g c", c=16)
        for d in range(2):
            dv(lambda e, d=d: e.tensor_tensor(out=v3(u1[:]), in0=v3(Br), in1=bc16(cr, d), op=ALU.mult))
            dv(lambda e, d=d: e.tensor_tensor(out=v3(u2[:]), in0=v3(Bi), in1=bc16(ci, d), op=ALU.mult))
            dv(lambda e: e.tensor_tensor(out=bpr[:], in0=u1[:], in1=u2[:], op=ALU.subtract))
            dv(lambda e, d=d: e.tensor_tensor(out=v3(u1[:]), in0=v3(Bi), in1=bc16(cr, d), op=ALU.mult))
            dv(lambda e, d=d: e.tensor_tensor(out=v3(u2[:]), in0=v3(Br), in1=bc16(ci, d), op=ALU.mult))
            dv(lambda e: e.tensor_tensor(out=bpi[:], in0=u1[:], in1=u2[:], op=ALU.add))
            dv(lambda e: e.tensor_scalar(out=u1[:], in0=bpi[:], scalar1=mhi[:, 0:1], scalar2=None, op0=ALU.mult))
            dv(lambda e, d=d: e.scalar_tensor_tensor(out=X1[:, d, :], in0=bpr[:], scalar=mlo[:, 0:1], in1=u1[:], op0=ALU.mult, op1=ALU.add))
            dv(lambda e: e.tensor_scalar(out=u1[:], in0=bpi[:], scalar1=mlo[:, 0:1], scalar2=None, op0=ALU.mult))
            dv(lambda e, d=d: e.scalar_tensor_tensor(out=X2[:, d, :], in0=bpr[:], scalar=mhi[:, 0:1], in1=u1[:], op0=ALU.mult, op1=ALU.subtract))
        Y1, Y2 = T(256), T(256)
        dv(lambda e: e.tensor_scalar(out=u1[:], in0=Ci, scalar1=mhi[:, 0:1], scalar2=None, op0=ALU.mult))
        dv(lambda e: e.scalar_tensor_tensor(out=Y1[:], in0=Cr, scalar=mlo[:, 0:1], in1=u1[:], op0=ALU.mult, op1=ALU.subtract))
        dv(lambda e: e.tensor_scalar(out=u1[:], in0=Cr, scalar1=mhi[:, 0:1], scalar2=-1.0, op0=ALU.mult, op1=ALU.mult))
        dv(lambda e: e.tensor_scalar(out=u2[:], in0=Ci, scalar1=mlo[:, 0:1], scalar2=None, op0=ALU.mult))
        dv(lambda e: e.tensor_tensor(out=Y2[:], in0=u1[:], in1=u2[:], op=ALU.subtract))
        BX = k.sb(st, [128, 2, 8, 256])
        CX = k.sb(st, [128, 2, 9, 256])

        def pw16(t, e_, d):
            return sbap(t, 0, 128, e_ * W + d * 16, [[1, 16], [0, 16]])

        for d in range(2):
            for e_ in range(9):
                if e_ < 8:
                    dv(lambda e, d=d, e_=e_: e.tensor_tensor(out=v3(u1[:]), in0=v3(X1[:, d, :]), in1=pw16(PWr, e_, d), op=ALU.mult))
                    dv(lambda e, d=d, e_=e_: e.tensor_tensor(out=v3(u2[:]), in0=v3(X2[:, d, :]), in1=pw16(PWi, e_, d), op=ALU.mult))
                    dv(lambda e, d=d, e_=e_: e.tensor_tensor(out=BX[:, d, e_, :], in0=u1[:], in1=u2[:], op=ALU.add))
                dv(lambda e, d=d, e_=e_: e.tensor_tensor(out=v3(u1[:]), in0=v3(Y1[:]), in1=pw16(PWr, e_, d), op=ALU.mult))
                dv(lambda e, d=d, e_=e_: e.tensor_tensor(out=v3(u2[:]), in0=v3(Y2[:]), in1=pw16(PWi, e_, d), op=ALU.mult))
                dv(lambda e, d=d, e_=e_: e.tensor_tensor(out=CX[:, d, e_, :], in0=u1[:], in1=u2[:], op=ALU.add))
        Jm = k.sb(st, [128, 128])
        dv(lambda e: e.tensor_copy(out=Jm[:, 0:64], in_=k.ident[:, 64:128]), r=[k.ident_b])
        dv(lambda e: e.tensor_copy(out=Jm[:, 64:128], in_=k.ident[:, 0:64]), r=[k.ident_b])
        NBF = 2
        Ug = [k.sb(st, [16, LT])] * NBF
        bU = [Buf()] * NBF
        MT = [k.sb(st, [16, 2, 8, 128]) for _ in range(NBF)]
        bMT = [Buf() for _ in range(NBF)]
        TL = [k.sb(st, [16, 15, 16]) for _ in range(NBF)]
        bTL = [Buf() for _ in range(NBF)]
        LG = [k.sb(st, [16, 256]) for _ in range(NBF)]
        bLG = [Buf() for _ in range(NBF)]
        AM = [k.sb(st, [128, 128]) for _ in range(4)]
        bAM = [Buf() for _ in range(4)]
        HH = [[k.sb(st, [128, NB + 1]) for _ in range(2)] for _ in range(2)]
        bHH = [[Buf(), Buf()], [Buf(), Buf()]]
        YG = [k.sb(st, [16, LT])] * NBF
        bYG = [Buf()] * NBF
        pX = [k.ps(st) for _ in range(2)]
        bpX = [Buf(), Buf()]
        pSn = [k.ps(st) for _ in range(2)]
        bpSn = [Buf(), Buf()]
        pY = [k.ps(st) for _ in range(2)]
        bpY = [Buf(), Buf()]
        pM = k.ps(st)
        bpM = Buf()
        pL = k.ps(st)
        bpL = Buf()
        segs = [(S, ncb, 0, nlb), (0, nlb, ncb, 0)]
        xc = [0]
        amc = [0]

        def ntiles(nb):
            return [(b0, min(512, nb - b0)) for b0 in range(0, nb, 512)]

        for g in range(16):
            gi = g % NBF
            sc.dma("sp", Ug[gi][:], k.UT.ap()[U_S5 + 16 * g:U_S5 + 16 * (g + 1), :], r=[k.dbuf["UT"]], w=[bU[gi]])
            for d in range(2):
                for e_ in range(8):
                    sc.op("pe", lambda e, d=d, e_=e_, g=g: e.transpose(out=pM[0:16, (d * 8 + e_) % 4 * 128:((d * 8 + e_) % 4 + 1) * 128],
                                                                      in_=BX[:, d, e_, g * 16:(g + 1) * 16], identity=k.ident[:]),
                          r=[bq, k.ident_b], w=[bpM])
                    if (d * 8 + e_) % 4 == 3:
                        e0 = e_ - 3
                        sc.op("act", lambda e, d=d, e0=e0, gi=gi: e.activation(out=MT[gi][:, d, e0:e0 + 4, :].rearrange("p a b -> p (a b)"), in_=pM[0:16, :], func=AF.Copy),
                              r=[bpM], w=[bMT[gi], bpM])
            for d in range(2):
                for dd in range(8):
                    sc.op("pe", lambda e, d=d, dd=dd, g=g: e.matmul(pL[0:16, (d * 8 + dd) * 16:(d * 8 + dd + 1) * 16], lhsT=BX[:, d, dd, g * 16:(g + 1) * 16],
                                                                    rhs=CX[:, d, 0, g * 16:(g + 1) * 16], start=True, stop=True), r=[bq], w=[bpL])
            sc.op("act", lambda e, gi=gi: e.activation(out=LG[gi][:], in_=pL[0:16, 0:256], func=AF.Copy), r=[bpL], w=[bLG[gi], bpL])
            sc.op("dve", lambda e, gi=gi: e.tensor_copy(out=TL[gi][:, 8:15, :].rearrange("p a b -> p (a b)"), in_=LG[gi][:, 16:128]), r=[bLG[gi]], w=[bTL[gi]])
            for dd in range(1, 8):
                sc.op("dve", lambda e, gi=gi, dd=dd: e.tensor_copy(out=TL[gi][:, 7 - dd, :], in_=LG[gi][:, (8 + dd) * 16:(9 + dd) * 16]), r=[bLG[gi]], w=[bTL[gi]])
            sc.op("dve", lambda e, gi=gi: e.tensor_tensor(out=TL[gi][:, 7, :], in0=LG[gi][:, 0:16], in1=LG[gi][:, 128:144], op=ALU.add), r=[bLG[gi]], w=[bTL[gi]])
            sc.op("dve", lambda e, gi=gi, g=g: e.scalar_tensor_tensor(out=TL[gi][:, 7, :], in0=k.ident[0:16, 0:16], scalar=Dg[:, g:g + 1], in1=TL[gi][:, 7, :],
                                                                      op0=ALU.mult, op1=ALU.add), r=[bTL[gi], bpar, k.ident_b], w=[bTL[gi]])
            for d in range(2):
                H0 = HH[d][0]
                zc = 0 if d == 0 else NB
                sc.op("pool", lambda e, H0=H0, zc=zc: e.memset(H0[:, zc:zc + 1], 0.0), w=[bHH[d][0]])
                sc.op("pool", lambda e, d=d, zc=zc: e.memset(HH[d][1][:, zc:zc + 1], 0.0), w=[bHH[d][1]])
                for (c0, nb, fpos, bpos) in segs:
                    pos0 = (fpos + 1) if d == 0 else bpos
                    for (b0, bn) in ntiles(nb):
                        xi = xc[0] % 2
                        xc[0] += 1
                        for j in range(8):
                            e_ = (7 - j) if d == 0 else j
                            rhs = sbap(Ug[gi], 0, 16, c0 + 8 * b0 + j, [[8, bn]])
                            sc.op("pe", lambda e, xi=xi, bn=bn, gi=gi, d=d, e_=e_, rhs=rhs, j=j: e.matmul(pX[xi][:, 0:bn], lhsT=MT[gi][:, d, e_, :], rhs=rhs,
                                                                                                      start=(j == 0), stop=(j == 7)),
                                  r=[bMT[gi], bU[gi]], w=[bpX[xi]], inc=(j == 7))
                        sc.op("act", lambda e, xi=xi, bn=bn, H0=H0, p=pos0 + b0: e.activation(out=H0[:, p:p + bn], in_=pX[xi][:, 0:bn], func=AF.Copy),
                              r=[bpX[xi]], w=[bHH[d][0]])
            cur = [0, 0]
            for s_ in range(nsteps):
                sh = 1 << s_
                if sh >= NB:
                    break
                for d in range(2):
                    ai = amc[0] % 4
                    amc[0] += 1
                    col = d * 16 + g
                    sc.op("dve", lambda e, ai=ai, s_=s_, col=col: e.tensor_scalar(out=AM[ai][:], in0=k.ident[:], scalar1=SPr[:, s_, col:col + 1], scalar2=None, op0=ALU.mult),
                          r=[bq, k.ident_b], w=[bAM[ai]])
                    sc.op("dve", lambda e, ai=ai, s_=s_, col=col: e.scalar_tensor_tensor(out=AM[ai][:], in0=Jm[:], scalar=SPj[:, s_, col:col + 1], in1=AM[ai][:], op0=ALU.mult, op1=ALU.add),
                          r=[bq, bAM[ai]], w=[bAM[ai]])
                    src, dst = HH[d][cur[d]], HH[d][1 - cur[d]]
                    bsrc, bdst = bHH[d][cur[d]], bHH[d][1 - cur[d]]
                    off = 1 if d == 0 else 0
                    n_sh = NB - sh
                    for (b0, bn) in ntiles(n_sh):
                        xi = xc[0] % 2
                        xc[0] += 1
                        if d == 0:
                            rcol, wcol = off + b0, off + sh + b0
                        else:
                            rcol, wcol = off + sh + b0, off + b0
                        sc.op("pe", lambda e, xi=xi, bn=bn, ai=ai, src=src, rcol=rcol: e.matmul(pSn[xi][:, 0:bn], lhsT=AM[ai][:], rhs=src[:, rcol:rcol + bn], start=True, stop=True),
                              r=[bAM[ai], bsrc], w=[bpSn[xi]])
                        sc.op("dve", lambda e, xi=xi, bn=bn, src=src, dst=dst, wcol=wcol: e.tensor_tensor(out=dst[:, wcol:wcol + bn], in0=src[:, wcol:wcol + bn], in1=pSn[xi][:, 0:bn], op=ALU.add),
                              r=[bpSn[xi], bsrc], w=[bdst])
                    ucol = off if d == 0 else off + n_sh
                    sc.op("act", lambda e, src=src, dst=dst, ucol=ucol, sh=sh: e.activation(out=dst[:, ucol:ucol + sh], in_=src[:, ucol:ucol + sh], func=AF.Copy),
                          r=[bsrc], w=[bdst])
                    cur[d] = 1 - cur[d]
            HF, HB = HH[0][cur[0]], HH[1][cur[1]]
            bHF, bHB = bHH[0][cur[0]], bHH[1][cur[1]]
            for (c0, nb, fpos, bpos) in segs:
                for (b0, bn) in ntiles(nb):
                    for j in range(8):
                        yi = xc[0] % 2
                        xc[0] += 1
                        for i_ in range(8):
                            rhs = sbap(Ug[gi], 0, 16, c0 + 8 * b0 + i_, [[8, bn]])
                            sc.op("pe", lambda e, yi=yi, bn=bn, gi=gi, j=j, i_=i_, rhs=rhs: e.matmul(pY[yi][0:16, 0:bn], lhsT=TL[gi][:, j - i_ + 7, :], rhs=rhs, start=(i_ == 0), stop=False),
                                  r=[bTL[gi], bU[gi]], w=[bpY[yi]], inc=False)
                        sc.op("pe", lambda e, yi=yi, bn=bn, j=j, g=g, c=fpos + b0: e.matmul(pY[yi][0:16, 0:bn], lhsT=CX[:, 0, j + 1, g * 16:(g + 1) * 16], rhs=HF[:, c:c + bn], start=False, stop=False),
                              r=[bq, bHF], w=[bpY[yi]], inc=False)
                        sc.op("pe", lambda e, yi=yi, bn=bn, j=j, g=g, c=bpos + b0 + 1: e.matmul(pY[yi][0:16, 0:bn], lhsT=CX[:, 1, 8 - j, g * 16:(g + 1) * 16], rhs=HB[:, c:c + bn], start=False, stop=True),
                              r=[bq, bHB], w=[bpY[yi]], inc=True)
                        dst = sbap(YG[gi], 0, 16, c0 + 8 * b0 + j, [[8, bn]])
                        sc.op("act", lambda e, yi=yi, bn=bn, dst=dst: e.activation(out=dst, in_=pY[yi][0:16, 0:bn], func=AF.Copy), r=[bpY[yi]], w=[bYG[gi]])
            sc.dma("sp", k.YT.ap()[256 + 16 * g:256 + 16 * (g + 1), :], YG[gi][:], r=[bYG[gi]], w=[k.dbuf["YT"]])
        sc.barrier()
        sc.flush()


def phase_s5post(k, l):
    cfg, sc = k.cfg, k.sc
    LT = cfg.LT
    T = 512
    yv = k.YT.ap()[256:512, :].rearrange("(c p) t -> p c t", p=128)
    with ExitStack() as st:
        Wg = k.sb(st, [128, 2, 1024], BF16)
        bW = Buf()
        cl = CastLoader(k, st, 1024)
        for cc in range(2):
            cl.load(Wg[:, cc, :], k.s5glu.ap()[l, cc * 128:(cc + 1) * 128, :], 128, 1024, k.dbuf["s5glu_q"], bW)
        Y = [k.sb(st, [128, 2, T]) for _ in range(2)]
        bY = [Buf(), Buf()]
        t1 = [k.sb(st, [128, 2, T]) for _ in range(2)]
        bt1 = [Buf(), Buf()]
        Gb = [k.sb(st, [128, 2, T], BF16) for _ in range(2)]
        bGb = [Buf(), Buf()]
        o = [k.sb(st, [128, T]) for _ in range(2)]
        bo = [Buf(), Buf()]
        pb = [k.ps(st) for _ in range(4)]
        pbb = [Buf() for _ in range(4)]
        oc = 0
        C0, C1 = 0.044715, 2.0 * math.sqrt(2.0 / PI)
        for it, c0 in enumerate(range(0, LT, T)):
            tn = min(T, LT - c0)
            i = it % 2
            sc.dma("sp", Y[i][:, :, 0:tn], yv[:, :, c0:c0 + tn], r=[k.dbuf["YT"]], w=[bY[i]])
            sc.op("act", lambda e, i=i, tn=tn: e.activation(out=t1[i][:, :, 0:tn], in_=Y[i][:, :, 0:tn], func=AF.Square), r=[bY[i]], w=[bt1[i]])
            sc.op("dve", lambda e, i=i, tn=tn: e.tensor_scalar(out=t1[i][:, :, 0:tn], in0=t1[i][:, :, 0:tn], scalar1=C0, scalar2=1.0, op0=ALU.mult, op1=ALU.add),
                  r=[bt1[i]], w=[bt1[i]])
            sc.op("dve", lambda e, i=i, tn=tn: e.tensor_tensor(out=t1[i][:, :, 0:tn], in0=t1[i][:, :, 0:tn], in1=Y[i][:, :, 0:tn], op=ALU.mult), r=[bt1[i], bY[i]], w=[bt1[i]])
            sc.op("act", lambda e, i=i, tn=tn: e.activation(out=t1[i][:, :, 0:tn], in_=t1[i][:, :, 0:tn], func=AF.Sigmoid, scale=C1), r=[bt1[i]], w=[bt1[i]])
            sc.op("dve", lambda e, i=i, tn=tn: e.tensor_tensor(out=Y[i][:, :, 0:tn], in0=Y[i][:, :, 0:tn], in1=t1[i][:, :, 0:tn], op=ALU.mult), r=[bt1[i], bY[i]], w=[bY[i]])
            sc.op("pool", lambda e, i=i, tn=tn: e.tensor_copy(out=Gb[i][:, :, 0:tn], in_=Y[i][:, :, 0:tn]), r=[bY[i]], w=[bGb[i]])
            sc.dma("sp", yv[:, :, c0:c0 + tn], Y[i][:, :, 0:tn], r=[bY[i]], w=[k.dbuf["YT"]])
            for ot in range(8):
                pi = oc % 4
                oi = oc % 2
                oc += 1
                for cc in range(2):
                    sc.op("pe", lambda e, pi=pi, cc=cc, ot=ot, i=i, tn=tn: e.matmul(pb[pi][:, 0:tn], lhsT=Wg[:, cc, ot * 128:(ot + 1) * 128], rhs=Gb[i][:, cc, 0:tn],
                                                                               start=(cc == 0), stop=(cc == 1)), r=[bW, bGb[i]], w=[pbb[pi]], inc=(cc == 1))
                sc.op("act", lambda e, pi=pi, oi=oi, tn=tn: e.activation(out=o[oi][:, 0:tn], in_=pb[pi][:, 0:tn], func=AF.Copy), r=[pbb[pi]], w=[bo[oi]])
                sc.dma("sp", k.GLUP.ap()[ot * 128:(ot + 1) * 128, c0:c0 + tn], o[oi][:, 0:tn], r=[bo[oi]], w=[k.dbuf["GLUP"]])
        sc.cc("ReduceScatter", ALU.add, GROUPS4, k.GLUP.ap().opt(), k.GLUL.ap().opt(), r=[k.dbuf["GLUP"]], w=[k.dbuf["GLU_loc"]])
        gv = k.GLUL.ap().rearrange("(c p) t -> p c t", p=128)
        zb = [k.sb(st, [128, 2, T], BF16) for _ in range(2)]
        bzb = [Buf(), Buf()]
        for it, c0 in enumerate(range(0, LT, T)):
            tn = min(T, LT - c0)
            i = it % 2
            sc.dma("sp", Y[i][:, :, 0:tn], yv[:, :, c0:c0 + tn], r=[k.dbuf["YT"]], w=[bY[i]])
            sc.dma("sp", t1[i][:, :, 0:tn], gv[:, :, c0:c0 + tn], r=[k.dbuf["GLU_loc"]], w=[bt1[i]])
            sc.op("act", lambda e, i=i, tn=tn: e.activation(out=t1[i][:, :, 0:tn], in_=t1[i][:, :, 0:tn], func=AF.Sigmoid), r=[bt1[i]], w=[bt1[i]])
            sc.op("dve", lambda e, i=i, tn=tn: e.tensor_tensor(out=zb[i][:, :, 0:tn], in0=Y[i][:, :, 0:tn], in1=t1[i][:, :, 0:tn], op=ALU.mult), r=[bt1[i], bY[i]], w=[bzb[i]])
            sc.dma("sp", k.ZT.ap()[256:512, c0:c0 + tn].rearrange("(c p) t -> p c t", p=128), zb[i][:, :, 0:tn], r=[bzb[i]], w=[k.dbuf["ZT"]])
        sc.barrier()
        sc.flush()


def phase_out(k):
    cfg, sc = k.cfg, k.sc
    D, KD, SL = cfg.D, cfg.KD, cfg.SL
    out = k.dout("out", [SL, D])
    hv = k.HT.ap().rearrange("(c p) t -> p c t", p=128)
    with ExitStack() as st:
        hin = [k.sb(st, [128, KD, 128]) for _ in range(2)]
        bhin = [Buf(), Buf()]
        ot = [k.sb(st, [128, D]) for _ in range(2)]
        bot = [Buf(), Buf()]
        pb = [k.ps(st) for _ in range(4)]
        pbb = [Buf() for _ in range(4)]
        pc = 0
        for t in range(SL // 128):
            i = t % 2
            sc.dma("sp", hin[i][:], hv[:, :, t * 128:(t + 1) * 128], r=[k.dbuf["HT_loc"]], w=[bhin[i]])
            for k0 in range(0, KD, 4):
                pi = pc % 4
                pc += 1
                for j in range(4):
                    sc.op("pe", lambda e, pi=pi, i=i, kc=k0 + j, j=j: e.transpose(out=pb[pi][:, j * 128:(j + 1) * 128], in_=hin[i][:, kc, :], identity=k.ident[:]),
                          r=[bhin[i], k.ident_b], w=[pbb[pi]])
                if pc % 2:
                    sc.op("act", lambda e, pi=pi, i=i, k0=k0: e.activation(out=ot[i][:, k0 * 128:(k0 + 4) * 128], in_=pb[pi][:, :], func=AF.Copy), r=[pbb[pi]], w=[bot[i]])
                else:
                    sc.op("dve", lambda e, pi=pi, i=i, k0=k0: e.tensor_copy(out=ot[i][:, k0 * 128:(k0 + 4) * 128], in_=pb[pi][:, :]), r=[pbb[pi]], w=[bot[i]])
            sc.dma("sp", out.ap()[t * 128:(t + 1) * 128, :], ot[i][:], r=[bot[i]], w=[k.dbuf["out"]])
        sc.barrier()
        sc.flush()


def build_program(cfg):
    k = K(cfg)
    make_ident(k)
    phase_adaln(k)
    load_small(k)
    phase_x(k)
    phase_G(k)
    mixer_setup(k)
    ssd_setup(k)
    s5_setup(k)
    stop = int(os.environ.get("K_STOP", "999"))
    n = 0
    for l in range(cfg.depth):
        for ph in (phase_A, phase_conv, phase_fnet, phase_ssd, phase_s5, phase_s5post, phase_wout, phase_C1, phase_FFN, phase_C3):
            if n < stop:
                ph(k, l)
            n += 1
    phase_out(k)
    return k


def run_cfg(cfg, inputs):
    k = build_program(cfg)
    need = list(k.dram.keys())
    maps = host_inputs(cfg, inputs, need)
    res = run_bass_kernel_spmd(k.nc, maps, core_ids=list(range(8)))
    out = np.zeros((2, cfg.S, cfg.D), np.float32)
    for c in range(8):
        b, q = c // 4, c % 4
        out[b, q * cfg.SL:(q + 1) * cfg.SL] = res.results[c]["out"]
    return out


def kernel(**inputs):
    inputs = {n: np.asarray(v) for n, v in inputs.items()}
    return run_cfg(Cfg(), inputs)
```
